# Optimizing a Trainium2 kernel written in Bass

```python
import jax, jax.numpy as jnp
from jax import lax
import numpy as np

D_MODEL = 4096
BATCH = 1
SEQ = 16384
DEPTH = 1
DEC_BATCH = 32
DEC_SEQ = 64
PAST_LEN = 4096

CHUNK = 64
SGU_CHUNK = 128
A_WIDTH = D_MODEL // 2
A_HEADS = 16
A_HEAD_DIM = A_WIDTH // A_HEADS
B_WIDTH = D_MODEL - A_WIDTH
B_HEADS = 4
B_KEY_DIM = B_WIDTH // 2
B_HEAD_K = B_KEY_DIM // B_HEADS
B_HEAD_V = B_WIDTH // B_HEADS
GATE_RANK = 16
GATE_TAU = 16.0
D_FF = -(-8 * D_MODEL // (3 * 256)) * 256
EPS = 1e-6
IN_COLS = 2 * A_WIDTH + 2 * B_KEY_DIM + 2 * B_WIDTH + GATE_RANK

kernel_name = "hybrid_sgu_gla_stream_step"


def rmsnorm(x, g):
    xf = x.astype(jnp.float32)
    y = xf * lax.rsqrt(jnp.mean(xf * xf, axis=-1, keepdims=True) + EPS)
    return (y * g.astype(jnp.float32)).astype(x.dtype)


def layernorm(x, g, b):
    xf = x.astype(jnp.float32)
    mu = jnp.mean(xf, axis=-1, keepdims=True)
    var = jnp.mean(jnp.square(xf - mu), axis=-1, keepdims=True)
    y = (xf - mu) * lax.rsqrt(var + EPS) * g.astype(jnp.float32) + b.astype(jnp.float32)
    return y.astype(x.dtype)


def sgu(z_u, z_v, w_s, b_s, ln_g, ln_b):
    bsz, L, _ = z_u.shape
    c = min(SGU_CHUNK, L)
    n = L // c
    v_n = layernorm(z_v, ln_g, ln_b)
    vv = v_n.reshape(bsz, n, c, A_HEADS, A_HEAD_DIM)
    w = w_s[:, :c, :c] * jnp.tril(jnp.ones((c, c), w_s.dtype))
    mixed = jnp.einsum('hij,bnjhd->bnihd', w, vv) + b_s[:, :c].T[None, None, :, :, None]
    return z_u * mixed.reshape(bsz, L, A_WIDTH), v_n


def gla(q, k, v, log_a, S0):
    bsz, L, H, dk = q.shape
    dv = v.shape[-1]
    c = min(CHUNK, L)
    n = L // c

    def to_blocks(t):
        return t.reshape(bsz, n, c, H, t.shape[-1]).transpose(1, 0, 3, 2, 4)

    mask = jnp.tril(jnp.ones((c, c), dtype=bool))
    ref = c // 2

    def step(S, inp):
        qc, kc, vc, gc = inp
        b = jnp.cumsum(gc, axis=2)
        b_ref = b[:, :, ref:ref + 1]
        b_last = b[:, :, -1:]
        att = jnp.einsum('bhid,bhjd->bhij', qc * jnp.exp(b - b_ref), kc * jnp.exp(b_ref - b))
        att = jnp.where(mask, att, 0.0)
        o = jnp.einsum('bhij,bhje->bhie', att, vc) + jnp.einsum('bhid,bhde->bhie', qc * jnp.exp(b), S)
        S_new = jnp.exp(b_last)[:, :, 0, :, None] * S + jnp.einsum('bhjd,bhje->bhde', kc * jnp.exp(b_last - b), vc)
        return S_new, o

    xs = (to_blocks(q), to_blocks(k), to_blocks(v), to_blocks(log_a))
    S_fin, o = lax.scan(step, S0.astype(jnp.float32), xs)
    o = o.transpose(1, 0, 3, 2, 4).reshape(bsz, L, H, dv)
    return o.astype(q.dtype), S_fin.astype(S0.dtype)


def mixer(h, S0, w_in, w_s, b_s, ln_g, ln_b, w_gate_up, b_gate, gla_norm_g, w_out):
    bsz, L, _ = h.shape
    proj = jnp.einsum('bld,de->ble', h, w_in)
    offs = np.cumsum([0, A_WIDTH, A_WIDTH, B_KEY_DIM, B_KEY_DIM, B_WIDTH, B_WIDTH, GATE_RANK])
    a_u, a_v, q, k, v, r, g_lr = [proj[..., int(offs[i]):int(offs[i + 1])] for i in range(7)]
    a_out, v_rows = sgu(jax.nn.gelu(a_u), jax.nn.gelu(a_v), w_s, b_s, ln_g, ln_b)
    q = q.reshape(bsz, L, B_HEADS, B_HEAD_K) * (B_HEAD_K ** -0.5)
    k = k.reshape(bsz, L, B_HEADS, B_HEAD_K)
    v = v.reshape(bsz, L, B_HEADS, B_HEAD_V)
    z = (jnp.einsum('blr,rk->blk', g_lr, w_gate_up) + b_gate).astype(jnp.float32)
    log_a = (jax.nn.log_sigmoid(z) / GATE_TAU).reshape(bsz, L, B_HEADS, B_HEAD_K)
    o, S_new = gla(q, k, v, log_a, S0)
    o = rmsnorm(o, gla_norm_g).reshape(bsz, L, B_WIDTH)
    b_out = o * jax.nn.silu(r)
    mix = jnp.einsum('ble,ed->bld', jnp.concatenate([a_out, b_out], axis=-1), w_out)
    return mix, v_rows, S_new


def swiglu(h, w_gate, w_up, w_down):
    return jnp.einsum('blf,fd->bld', jax.nn.silu(h @ w_gate) * (h @ w_up), w_down)


def trunk(x, states, g_mix, w_in, w_s, b_s, ln_g, ln_b, w_gate_up, b_gate, gla_norm_g, w_out,
          g_ffn, w_ffn_gate, w_ffn_up, w_ffn_down, g_final):
    new_states, v_rows_all = [], []
    for d in range(DEPTH):
        mix, v_rows, S_new = mixer(rmsnorm(x, g_mix[d]), states[d], w_in[d], w_s[d], b_s[d], ln_g[d],
                                   ln_b[d], w_gate_up[d], b_gate[d], gla_norm_g[d], w_out[d])
        x = x + mix
        x = x + swiglu(rmsnorm(x, g_ffn[d]), w_ffn_gate[d], w_ffn_up[d], w_ffn_down[d])
        new_states.append(S_new)
        v_rows_all.append(v_rows)
    return rmsnorm(x, g_final), jnp.stack(new_states), jnp.stack(v_rows_all)


def setup_inputs(seed: int = 0) -> dict:
    key = jax.random.key(seed)
    ks = jax.random.split(key, 24)
    nrm = lambda k, shape, s: jax.random.normal(k, shape, jnp.float32) * s
    return {
        "x_prompt": nrm(ks[0], (BATCH, SEQ, D_MODEL), 1.0),
        "x_sample": nrm(ks[1], (DEC_BATCH, DEC_SEQ, D_MODEL), 1.0),
        "state_gla": nrm(ks[2], (DEPTH, DEC_BATCH, B_HEADS, B_HEAD_K, B_HEAD_V), 0.5),
        "g_mix": 1.0 + nrm(ks[3], (DEPTH, D_MODEL), 0.02),
        "w_in": nrm(ks[4], (DEPTH, D_MODEL, IN_COLS), D_MODEL ** -0.5),
        "w_s": nrm(ks[5], (DEPTH, A_HEADS, SGU_CHUNK, SGU_CHUNK), SGU_CHUNK ** -0.5),
        "b_s": nrm(ks[6], (DEPTH, A_HEADS, SGU_CHUNK), 0.1),
        "ln_g": 1.0 + nrm(ks[7], (DEPTH, A_WIDTH), 0.02),
        "ln_b": nrm(ks[8], (DEPTH, A_WIDTH), 0.02),
        "w_gate_up": nrm(ks[9], (DEPTH, GATE_RANK, B_KEY_DIM), GATE_RANK ** -0.5),
        "b_gate": nrm(ks[10], (DEPTH, B_KEY_DIM), 0.1),
        "gla_norm_g": 1.0 + nrm(ks[11], (DEPTH, B_HEAD_V), 0.02),
        "w_out": nrm(ks[12], (DEPTH, D_MODEL, D_MODEL), D_MODEL ** -0.5),
        "g_ffn": 1.0 + nrm(ks[13], (DEPTH, D_MODEL), 0.02),
        "w_ffn_gate": nrm(ks[14], (DEPTH, D_MODEL, D_FF), D_MODEL ** -0.5),
        "w_ffn_up": nrm(ks[15], (DEPTH, D_MODEL, D_FF), D_MODEL ** -0.5),
        "w_ffn_down": nrm(ks[16], (DEPTH, D_FF, D_MODEL), D_FF ** -0.5),
        "g_final": 1.0 + nrm(ks[17], (D_MODEL,), 0.02),
    }


def reference(x_prompt, x_sample, state_gla, g_mix, w_in, w_s, b_s, ln_g, ln_b, w_gate_up, b_gate,
              gla_norm_g, w_out, g_ffn, w_ffn_gate, w_ffn_up, w_ffn_down, g_final):
    params = (g_mix, w_in, w_s, b_s, ln_g, ln_b, w_gate_up, b_gate, gla_norm_g, w_out,
              g_ffn, w_ffn_gate, w_ffn_up, w_ffn_down, g_final)
    zero_state = jnp.zeros((DEPTH, x_prompt.shape[0], B_HEADS, B_HEAD_K, B_HEAD_V), state_gla.dtype)
    y_prompt, new_gla_prompt, _ = trunk(x_prompt, zero_state, *params)
    y_sample, new_gla_sample, new_sgu_v_sample = trunk(x_sample, state_gla, *params)
    return (y_prompt, y_sample, new_gla_prompt, new_gla_sample, new_sgu_v_sample)
```

```python
import math
from contextlib import ExitStack

import numpy as np
import concourse.bass as bass
import concourse.mybir as mybir
from concourse.bass_utils import run_bass_kernel_spmd

F32 = mybir.dt.float32
BF16 = mybir.dt.bfloat16
AF = mybir.ActivationFunctionType
ALU = mybir.AluOpType

ENGS = ("pe", "act", "dve", "pool", "sp")
EPS = 1e-6
GATE_TAU = 16.0


class Op:
    __slots__ = ("eng", "pos", "fn", "waits", "sig", "val", "dsem", "dval")

    def __init__(self, eng, pos, fn):
        self.eng, self.pos, self.fn = eng, pos, fn
        self.waits = []
        self.sig = False
        self.val = None
        self.dsem = None
        self.dval = None


class Prog:
    def __init__(self, same_engine_sync=True):
        self.ops = {e: [] for e in ENGS}
        self.res = {}
        self.seen = {e: {} for e in ENGS}
        self.dcount = {}
        self.same_engine_sync = same_engine_sync
        self.out_dsems = set()

    def _dep(self, op, prod):
        if prod is None or prod is op:
            return
        if prod.dsem is not None:
            key, v = ("d", prod.dsem), prod.dval
        else:
            if prod.eng == op.eng and (prod.eng == "pe" or not self.same_engine_sync):
                return
            key, v = ("e", prod.eng), prod.pos
        seen = self.seen[op.eng]
        if seen.get(key, -1) >= v:
            return
        seen[key] = v
        if prod.dsem is not None:
            op.waits.append((prod.dsem, prod.dval))
        else:
            prod.sig = True
            op.waits.append(prod)

    def op(self, eng, fn, reads=(), writes=(), dsem=None, is_output=False):
        lst = self.ops[eng]
        o = Op(eng, len(lst), fn)
        if dsem is not None:
            c = self.dcount.get(dsem, 0) + 1
            self.dcount[dsem] = c
            o.dsem, o.dval = dsem, c
            if is_output:
                self.out_dsems.add(dsem)
        prods = []
        for r in reads:
            st = self.res.setdefault(r, [None, []])
            prods.append(st[0])
        for w in writes:
            st = self.res.setdefault(w, [None, []])
            prods.append(st[0])
            prods.extend(st[1])
        prods = [p for p in prods if p is not None]
        prods.sort(key=lambda p: -(p.dval if p.dsem is not None else p.pos))
        for p in prods:
            self._dep(o, p)
        for r in reads:
            self.res[r][1].append(o)
        for w in writes:
            st = self.res[w]
            st[0] = o
            st[1] = []
        lst.append(o)
        return o

    def emit(self, nc, stack):
        esem = {e: stack.enter_context(nc.semaphore("es_" + e)) for e in ENGS}
        dsem = {n: stack.enter_context(nc.semaphore("ds_" + n)) for n in self.dcount}
        for e in ENGS:
            c = 0
            for o in self.ops[e]:
                if o.sig and o.dsem is None:
                    c += 1
                    o.val = c
        final_waits = [(n, self.dcount[n]) for n in sorted(self.out_dsems)]
        block = stack.enter_context(nc.Block())

        def run(engname):
            def body(eng):
                for o in self.ops[engname]:
                    for w in o.waits:
                        if isinstance(w, tuple):
                            eng.wait_ge(dsem[w[0]], 16 * w[1])
                        else:
                            eng.wait_ge(esem[w.eng], w.val)
                    ins = o.fn(eng)
                    if o.dsem is not None:
                        ins.then_inc(dsem[o.dsem], 16)
                    elif o.sig:
                        ins.then_inc(esem[engname], 1)
                if engname == "sp":
                    for n, c in final_waits:
                        eng.wait_ge(dsem[n], 16 * c)
            return body

        block.tensor(run("pe"))
        block.scalar(run("act"))
        block.vector(run("dve"))
        block.gpsimd(run("pool"))
        block.sync(run("sp"))


class Cfg:
    def __init__(self, D=4096, AH=16, BH=4, NPG=8, NSEQ=4, LBG=56, NCORES=8):
        self.D = D
        self.AW = D // 2
        self.AH = AH
        assert self.AW // AH == 128
        self.BH = BH
        self.BW = D - self.AW
        self.BK = self.BW // 2
        self.DK = self.BK // BH
        self.DV = self.BW // BH
        assert self.DK % 128 == 0 and self.DV == 512
        self.R = 16
        self.FF = -(-8 * D // (3 * 256)) * 256
        self.IN_COLS = 2 * self.AW + 2 * self.BK + 2 * self.BW + self.R
        self.TG = 256
        self.NT = 2
        self.KD = D // 128
        self.KC = self.DK // 128
        self.NPG = NPG
        self.NSEQ = NSEQ
        assert NSEQ * 64 == self.TG
        self.LBG = LBG
        self.NCORES = NCORES
        self.QOFF = 2 * self.AW
        self.KOFF = self.QOFF + self.BK
        self.VOFF = self.KOFF + self.BK
        self.ROFF = self.VOFF + self.BW
        self.GOFF = self.ROFF + self.BW
        self.NE = 3 * 128 + 2


def host_consts(cfg):
    j = np.arange(128)[:, None]
    i = np.arange(128)[None, :]
    same = (j // 64) == (i // 64)
    ident = np.eye(128, dtype=np.float32)
    maskP = (j <= i).astype(np.float32)
    maskS = ((j <= i) & same).astype(np.float32)
    s = -1.0 / GATE_TAU
    ref = (i // 64) * 64 + 32
    U1 = (((j <= i) & same).astype(np.float32) - ((j <= ref) & same).astype(np.float32)) * s
    Ub = ((j <= i) & same).astype(np.float32) * s
    U3 = ((j > i) & same).astype(np.float32) * s
    blk = np.zeros((128, 2), np.float32)
    blk[:64, 0] = s
    blk[64:, 1] = s
    ucat = np.concatenate([U1, -U1, Ub, blk], axis=1).astype(np.float32)
    return dict(c_ident=ident, c_maskP=maskP, c_maskS=maskS, c_ucat=ucat, c_u3=U3.astype(np.float32),
                c_ones=np.ones((1, cfg.TG), np.float32))


class MK:
    def __init__(self, cfg):
        self.c = cfg

    def build(self):
        c = self.c
        nc = bass.Bass("TRN2", target_bir_lowering=False)
        self.nc = nc
        D, AW, AH, BH, DK, DV, FF, TG, NT, KD, KC = c.D, c.AW, c.AH, c.BH, c.DK, c.DV, c.FF, c.TG, c.NT, c.KD, c.KC

        def din(name, shape):
            return nc.dram_tensor(name, list(shape), F32, kind="ExternalInput").ap()

        def dout(name, shape):
            return nc.dram_tensor(name, list(shape), F32, kind="ExternalOutput").ap()

        d = self.d = {}
        d["xp"] = din("xp", [c.NPG * TG, D])
        d["xs"] = din("xs", [TG, D])
        d["xpre"] = din("xpre", [max(c.LBG, 1) * TG, D])
        d["st"] = din("st", [c.NSEQ, BH, DK, DV])
        d["w_in"] = din("w_in", [D, c.IN_COLS])
        d["w_out"] = din("w_out", [D, D])
        d["w_fg"] = din("w_fg", [D, FF])
        d["w_fu"] = din("w_fu", [D, FF])
        d["w_fd"] = din("w_fd", [FF, D])
        d["gmixc"] = din("gmixc", [128, KD])
        d["gffnc"] = din("gffnc", [128, KD])
        d["gfin"] = din("gfin", [1, D])
        d["ln_g"] = din("ln_g", [1, AW])
        d["ln_b"] = din("ln_b", [1, AW])
        d["wgu"] = din("wgu", [17, BH * DK])
        d["gnorm"] = din("gnorm", [1, DV])
        d["wsT_p"] = din("wsT_p", [128, AH * 128])
        d["wsT_s"] = din("wsT_s", [128, AH * 128])
        d["bs_p"] = din("bs_p", [1, AH * 128])
        d["bs_s"] = din("bs_s", [1, AH * 128])
        d["c_ident"] = din("c_ident", [128, 128])
        d["c_maskP"] = din("c_maskP", [128, 128])
        d["c_maskS"] = din("c_maskS", [128, 128])
        d["c_ucat"] = din("c_ucat", [128, c.NE])
        d["c_u3"] = din("c_u3", [128, 128])
        d["c_ones"] = din("c_ones", [1, TG])
        d["yp"] = dout("yp", [c.NPG * TG, D])
        d["ys"] = dout("ys", [TG, D])
        d["sp_out"] = dout("sp_out", [BH, DK, DV])
        d["ss_out"] = dout("ss_out", [c.NSEQ, BH, DK, DV])
        d["vs_out"] = dout("vs_out", [TG, AW])

        self.P = Prog()
        with ExitStack() as st:
            self.st = st
            self.alloc()
            self.program()
            self.P.emit(nc, st)
        return nc

    def sb(self, name, shape, dt):
        return self.st.enter_context(self.nc.sbuf_tensor("s_" + name, list(shape), dt))

    def alloc(self):
        c = self.c
        D, AW, AH, BH, DK, DV, TG, NT, KD, KC = c.D, c.AW, c.AH, c.BH, c.DK, c.DV, c.TG, c.NT, c.KD, c.KC
        s = self.s = {}
        s["x1"] = self.sb("x1", [128, NT, D], F32)
        s["hT"] = self.sb("hT", [128, KD, TG], BF16)
        s["catT"] = self.sb("catT", [128, KD, TG], BF16)
        s["scr0"] = self.sb("scr0", [128, AW], F32)
        s["scr1"] = self.sb("scr1", [128, AW], F32)
        s["vn"] = self.sb("vn", [128, NT, AW], BF16)
        s["spg"] = self.sb("spg", [128, NT, BH * DK], F32)
        s["glrT"] = self.sb("glrT", [17, TG], F32)
        s["qT"] = self.sb("qT", [128, KC, TG], F32)
        s["kT"] = self.sb("kT", [128, KC, TG], F32)
        s["ktm"] = self.sb("ktm", [128, NT, DK], F32)
        s["vtm"] = self.sb("vtm", [128, NT, DV], BF16)
        s["rs"] = self.sb("rs", [128, NT, DV], BF16)
        s["E"] = self.sb("E", [128, KC, c.NE], F32)
        s["EB3"] = self.sb("EB3", [128, DK], F32)
        s["qeT"] = self.sb("qeT", [128, KC, 128], BF16)
        s["keT"] = self.sb("keT", [128, KC, 128], BF16)
        s["qbAB"] = self.sb("qbAB", [128, KC, 2, 128], BF16)
        s["kbAB"] = self.sb("kbAB", [128, 2, DK], BF16)
        s["attT"] = self.sb("attT", [128, 128], BF16)
        s["otmp"] = self.sb("otmp", [128, DV], F32)
        s["bout"] = self.sb("bout", [128, DV], BF16)
        s["stats"] = self.sb("stats", [128, 32], F32)
        s["S"] = self.sb("S", [128, BH, KC, DV], F32)
        s["Sbf"] = self.sb("Sbf", [128, KC, DV], BF16)
        s["actT0"] = self.sb("actT0", [128, 4, TG], BF16)
        s["actT1"] = self.sb("actT1", [128, 4, TG], BF16)
        s["gu0"] = self.sb("gu0", [128, TG], F32)
        s["gu1"] = self.sb("gu1", [128, TG], F32)
        s["tmpf"] = self.sb("tmpf", [128, TG], F32)
        s["gbc"] = self.sb("gbc", [128, D], F32)
        self.NW = 3
        for i in range(self.NW):
            s[f"w{i}"] = self.sb(f"w{i}", [128, 4096], BF16)
        s["ident"] = self.sb("ident", [128, 128], F32)
        s["identb"] = self.sb("identb", [128, 128], BF16)
        s["maskP"] = self.sb("maskP", [128, 128], F32)
        s["maskS"] = self.sb("maskS", [128, 128], F32)
        s["ucat"] = self.sb("ucat", [128, c.NE], F32)
        s["u3"] = self.sb("u3", [128, 128], F32)
        s["Wt"] = self.sb("Wt", [128, AH, 128], BF16)
        s["bsbc"] = self.sb("bsbc", [128, AH, 128], F32)
        s["gmixc"] = self.sb("gmixc", [128, KD], F32)
        s["gffnc"] = self.sb("gffnc", [128, KD], F32)
        s["wg"] = self.sb("wg", [128, KD, 16], BF16)
        s["wgu"] = self.sb("wgu", [17, BH * DK], F32)
        s["gnbc"] = self.sb("gnbc", [128, DV], F32)
        self.pb = [self.st.enter_context(self.nc.psum_tensor(f"pb{i}", [128, 512], F32)) for i in range(8)]
        self.rr = {"acc": 0, "tp": 0, "g": 0, "w": 0, "gu": 0, "actT": 0}

    def bankres(self, i):
        return [f"pb{i}a", f"pb{i}b"]

    def next_bank(self, pool):
        rng = {"acc": (0, 4), "tp": (4, 2), "g": (6, 2)}[pool]
        i = rng[0] + self.rr[pool] % rng[1]
        self.rr[pool] += 1
        return i

    def mm(self, out, lhsT, rhs, start, stop, reads, writes):
        self.P.op("pe", lambda e: e.matmul(out, lhsT=lhsT, rhs=rhs, start=start, stop=stop), reads, writes)

    def tr(self, out, in_, ident, reads, writes):
        self.P.op("pe", lambda e: e.transpose(out=out, in_=in_, identity=ident), reads, writes)

    def act(self, out, in_, func, reads, writes, **kw):
        self.P.op("act", lambda e: e.activation(out=out, in_=in_, func=func, **kw), reads, writes)

    def cp(self, eng, out, in_, reads, writes):
        if eng == "act":
            self.P.op("act", lambda e: e.copy(out=out, in_=in_), reads, writes)
        else:
            self.P.op(eng, lambda e: e.tensor_copy(out=out, in_=in_), reads, writes)

    def tt(self, out, in0, in1, op, reads, writes):
        self.P.op("dve", lambda e: e.tensor_tensor(out=out, in0=in0, in1=in1, op=op), reads, writes)

    def ts(self, out, in0, s1, s2, op0, op1, reads, writes):
        self.P.op("dve", lambda e: e.tensor_scalar(out=out, in0=in0, scalar1=s1, scalar2=s2, op0=op0, op1=op1), reads, writes)

    def stt(self, out, in0, scalar, in1, op0, op1, reads, writes):
        self.P.op("dve", lambda e: e.scalar_tensor_tensor(out=out, in0=in0, scalar=scalar, in1=in1, op0=op0, op1=op1), reads, writes)

    def recip(self, out, in_, reads, writes):
        self.P.op("dve", lambda e: e.reciprocal(out=out, in_=in_), reads, writes)

    def dma(self, q, out, in_, reads, writes, dsem, is_output=False):
        return self.P.op(q, lambda e: e.dma_start(out=out, in_=in_), reads, writes, dsem=dsem, is_output=is_output)

    def memset(self, eng, ap, val, writes):
        self.P.op(eng, lambda e: e.memset(ap, val), (), writes)

    def wload(self, wd, row0, nkc, colranges):
        ncols = sum(n for _, n in colranges)
        assert nkc * ncols <= 4096
        si = self.rr["w"] % self.NW
        self.rr["w"] += 1
        buf = self.s[f"w{si}"]
        view = buf[:, 0:nkc * ncols].rearrange("p (k n) -> p k n", k=nkc)
        off = 0
        assert len(colranges) <= 2
        for idx, (c0, n) in enumerate(colranges):
            src = wd[row0:row0 + nkc * 128, c0:c0 + n].rearrange("(k p) n -> p k n", p=128)
            if len(colranges) == 1:
                wr = [f"w{si}", f"w{si}b"]
            else:
                wr = [f"w{si}"] if idx == 0 else [f"w{si}b"]
            self.dma("pool", view[:, :, off:off + n], src, reads=(), writes=wr, dsem=f"w{si}")
            off += n
        return view, [f"w{si}", f"w{si}b"]

    @staticmethod
    def _offs(colranges):
        out, off = [], 0
        for (_, n) in colranges:
            if off:
                out.append(off)
            off += n
        return out

    def tm_block(self, src, src_res, nk_total, wd, row0, colranges, evac, unit_k=8, use_cols=None):
        c = self.c
        ncols = sum(n for _, n in colranges)
        uoff, ucols = use_cols if use_cols else (0, ncols)
        banks = [self.next_bank("acc") for _ in range(c.NT)]
        for u0 in range(0, nk_total, unit_k):
            nk = min(unit_k, nk_total - u0)
            wv, wres = self.wload(wd, row0 + u0 * 128, nk, colranges)
            for t in range(c.NT):
                for kc in range(nk):
                    k = u0 + kc
                    self.mm(self.pb[banks[t]][:, 0:ucols], src(k, t), wv[:, kc, uoff:uoff + ucols], k == 0, k == nk_total - 1,
                            reads=src_res + wres, writes=self.bankres(banks[t]))
        for t in range(c.NT):
            evac(t, self.pb[banks[t]][:, 0:ucols], self.bankres(banks[t]))

    def fm_block(self, src, src_res, nk_total, wd, row0, colranges, evac, unit_k=8, chunks=None):
        c = self.c
        ncols = sum(n for _, n in colranges)
        chunks = list(range(ncols // 128)) if chunks is None else chunks
        nj = len(chunks)
        halves = []
        for jj in range(0, nj, 2):
            b = self.next_bank("acc")
            halves.append((b, 0))
            if jj + 1 < nj:
                halves.append((b, 1))
        for u0 in range(0, nk_total, unit_k):
            nk = min(unit_k, nk_total - u0)
            wv, wres = self.wload(wd, row0 + u0 * 128, nk, colranges)
            for j in range(nj):
                b, h = halves[j]
                for kc in range(nk):
                    k = u0 + kc
                    self.mm(self.pb[b][:, h * 256:h * 256 + c.TG], wv[:, kc, chunks[j] * 128:(chunks[j] + 1) * 128], src(k), (k == 0 and h == 0), k == nk_total - 1,
                            reads=src_res + wres, writes=self.bankres(b))
        for j in range(nj):
            b, h = halves[j]
            evac(j, self.pb[b][:, h * 256:h * 256 + c.TG], self.bankres(b))

    def program(self):
        import os
        c = self.c
        self.stage = int(os.environ.get("MK_STAGE", "99"))
        self.setup()
        if self.stage <= 0:
            return
        for g in range(c.LBG):
            self.group(self.d["xpre"][g * c.TG:(g + 1) * c.TG, :], mode="pre", ydst=None)
        for g in range(c.NPG):
            self.group(self.d["xp"][g * c.TG:(g + 1) * c.TG, :], mode="prompt", ydst=self.d["yp"][g * c.TG:(g + 1) * c.TG, :])
        for hh in range(c.BH):
            self.dma("sp", self.d["sp_out"][hh].rearrange("(k p) e -> p k e", p=128), self.s["S"][:, hh], reads=[f"S{hh}"], writes=(),
                     dsem=f"sp_out{hh}", is_output=True)
        self.set_mode_consts("s")
        self.group(self.d["xs"], mode="sample", ydst=self.d["ys"])

    def setup(self):
        c, s, d = self.c, self.s, self.d
        last = None
        cres = []
        for name in ["ident", "maskP", "maskS", "ucat", "u3", "gmixc", "gffnc", "wgu"]:
            src = {"ident": "c_ident", "maskP": "c_maskP", "maskS": "c_maskS", "ucat": "c_ucat", "u3": "c_u3"}.get(name, name)
            last = self.dma("sp", s[name][:], d[src], (), [name], dsem="const")
            cres.append(name)
        last = self.dma("sp", s["gnbc"][:], d["gnorm"].partition_broadcast(128), (), ["gnbc"], dsem="const")
        cres.append("gnbc")
        last = self.dma("sp", s["glrT"][16:17, :], d["c_ones"], (), ["glrT_ones"], dsem="const")
        cres.append("glrT_ones")
        for r in cres:
            self.P.res[r][0] = last
        for q in range(0, c.KD, 8):
            nk = min(8, c.KD - q)
            src = d["w_in"][q * 128:(q + nk) * 128, c.GOFF:c.GOFF + 16].rearrange("(k p) n -> p k n", p=128)
            self.dma("pool", s["wg"][:, q:q + nk, :], src, (), [f"wg{q}"], dsem="wg")
        self.wg_res = [f"wg{q}" for q in range(0, c.KD, 8)]
        self.cp("dve", s["identb"][:], s["ident"][:], ["ident"], ["identb"])
        self.memset("dve", s["qbAB"][:], 0.0, ["qbAB"])
        self.memset("dve", s["kbAB"][:], 0.0, ["kbAB"])
        for hh in range(c.BH):
            self.memset("dve", s["S"][:, hh], 0.0, [f"S{hh}"])
        self.set_mode_consts("p")

    def set_mode_consts(self, m):
        c, s, d = self.c, self.s, self.d
        self.dma("sp", s["scr0"][:], d["wsT_" + m], (), ["scr0"], dsem="scr0")
        mask = s["maskP"] if m == "p" else s["maskS"]
        self.tt(s["Wt"][:], s["scr0"][:].rearrange("p (h i) -> p h i", h=c.AH), mask[:].unsqueeze(1).to_broadcast([128, c.AH, 128]), ALU.mult,
                ["scr0", "maskP", "maskS"], ["Wt"])
        self.dma("sp", s["bsbc"][:].rearrange("p h i -> p (h i)"), d["bs_" + m].partition_broadcast(128), (), ["bsbc"], dsem="bsbc")

    def norm_T(self, t, gcol, gres, dstT, dres):
        c, s = self.c, self.s
        D, AW = c.D, c.AW
        st = s["stats"]
        x1r = f"x1_{t}"
        for hf in range(2):
            self.act(s[f"scr{hf}"][:], s["x1"][:, t, hf * AW:(hf + 1) * AW], AF.Square, [x1r], [f"scr{hf}", f"st{hf}"],
                     scale=1.0 / math.sqrt(D), accum_out=st[:, hf:hf + 1])
        self.tt(st[:, 2:3], st[:, 0:1], st[:, 1:2], ALU.add, ["st0", "st1"], ["st2"])
        self.act(st[:, 3:4], st[:, 2:3], AF.Sqrt, ["st2"], ["st3"], bias=EPS)
        self.recip(st[:, 4:5], st[:, 3:4], ["st3"], ["st4"])
        for hf in range(2):
            scr = s[f"scr{hf}"]
            self.act(scr[:], s["x1"][:, t, hf * AW:(hf + 1) * AW], AF.Copy, [x1r, "st4"], [f"scr{hf}"], scale=st[:, 4:5])
            nch = AW // 128
            for c0 in range(0, nch, 4):
                b = self.next_bank("tp")
                for j in range(4):
                    self.tr(self.pb[b][:, j * 128:(j + 1) * 128], scr[:, (c0 + j) * 128:(c0 + j + 1) * 128], s["ident"][:],
                            [f"scr{hf}", "ident"], self.bankres(b))
                cc = hf * nch + c0
                self.tt(dstT[:, cc:cc + 4, t * 128:(t + 1) * 128], self.pb[b][:].rearrange("p (a b) -> p a b", a=4),
                        gcol[:, cc:cc + 4].unsqueeze(2).to_broadcast([128, 4, 128]), ALU.mult,
                        self.bankres(b) + [gres], [dres])

    def group(self, xsrc, mode, ydst):
        c, s, d = self.c, self.s, self.d
        D, AW, AH, BH, DK, DV, FF, TG, NT, KD, KC = c.D, c.AW, c.AH, c.BH, c.DK, c.DV, c.FF, c.TG, c.NT, c.KD, c.KC
        full = mode != "pre"
        st = s["stats"]
        hT = s["hT"]
        for t in range(NT):
            self.dma("sp", s["x1"][:, t, :], xsrc[t * 128:(t + 1) * 128, :], (), [f"x1_{t}"], dsem=f"x1_{t}")
        if full:
            self.dma("sp", s["gbc"][:, 0:AW], d["ln_g"].partition_broadcast(128), (), ["gbc"], dsem="gbc")
            self.dma("sp", s["gbc"][:, AW:2 * AW], d["ln_b"].partition_broadcast(128), (), ["gbc2"], dsem="gbc2")
        for t in range(NT):
            self.norm_T(t, s["gmixc"], "gmixc", hT, "hT")
        b = self.next_bank("g")
        for k in range(KD):
            self.mm(self.pb[b][0:16, 0:TG], s["wg"][:, k, :], hT[:, k, :], k == 0, k == KD - 1, ["hT"] + self.wg_res, self.bankres(b))
        self.cp("act", s["glrT"][0:16, :], self.pb[b][0:16, 0:TG], self.bankres(b), ["glrT"])
        for t in range(NT):
            for nb in range(0, BH * DK, 512):
                b = self.next_bank("g")
                wz = min(512, BH * DK - nb)
                self.mm(self.pb[b][:, 0:wz], s["glrT"][0:17, t * 128:(t + 1) * 128], s["wgu"][0:17, nb:nb + wz], True, True,
                        ["glrT", "glrT_ones", "wgu"], self.bankres(b))
                self.act(s["spg"][:, t, nb:nb + wz], self.pb[b][:, 0:wz], AF.Exp, self.bankres(b), [f"spg{t}"], scale=-1.0)
                self.act(s["spg"][:, t, nb:nb + wz], s["spg"][:, t, nb:nb + wz], AF.Ln, [f"spg{t}"], [f"spg{t}"], bias=1.0)

        if self.stage <= 1:
            return
        srcT_tm = lambda k, t: hT[:, k, t * 128:(t + 1) * 128]
        srcT_fm = lambda k: hT[:, k, :]

        if full:
            for nb in range(AW // 512):
                def ev(t, ps, pres, nb=nb):
                    self.act(s[f"scr{t}"][:, nb * 512:(nb + 1) * 512], ps, AF.Gelu, pres, [f"scr{t}", f"avs{t}_{nb}"],
                             accum_out=st[:, 8 + t * 4 + nb:9 + t * 4 + nb])
                self.tm_block(srcT_tm, ["hT"], KD, d["w_in"], 0, [(AW + nb * 512, 512)], ev)
            for t in range(NT):
                scr = s[f"scr{t}"]
                nsum = AW // 512
                base = 8 + t * 4
                acc_res = [f"avs{t}_{nb}" for nb in range(nsum)]
                self.P.op("dve", lambda e, o=st[:, 16 + t:17 + t], i=st[:, base:base + nsum]: e.reduce_sum(out=o, in_=i, axis=mybir.AxisListType.X),
                          acc_res, [f"avm{t}"])
                self.P.op("dve", lambda e, o=st[:, 18 + t:19 + t], i=st[:, 16 + t:17 + t]: e.tensor_scalar_mul(out=o, in0=i, scalar1=-1.0 / AW),
                          [f"avm{t}"], [f"avn{t}"])
                self.act(s["vn"][:, t, :], scr[:], AF.Square, [f"scr{t}", f"avn{t}"], [f"vn{t}", f"avv{t}"],
                         bias=st[:, 18 + t:19 + t], scale=1.0, accum_out=st[:, 20 + t:21 + t])
                self.act(st[:, 22 + t:23 + t], st[:, 20 + t:21 + t], AF.Sqrt, [f"avv{t}"], [f"avsd{t}"], scale=1.0 / AW, bias=EPS)
                self.recip(st[:, 24 + t:25 + t], st[:, 22 + t:23 + t], [f"avsd{t}"], [f"avr{t}"])
                self.ts(scr[:], scr[:], st[:, 18 + t:19 + t], st[:, 24 + t:25 + t], ALU.add, ALU.mult, [f"scr{t}", f"avn{t}", f"avr{t}"], [f"scr{t}"])
                self.tt(scr[:], scr[:], s["gbc"][:, 0:AW], ALU.mult, [f"scr{t}", "gbc"], [f"scr{t}"])
                self.tt(scr[:], scr[:], s["gbc"][:, AW:2 * AW], ALU.add, [f"scr{t}", "gbc2"], [f"scr{t}"])
                if mode == "sample":
                    self.dma("sp", d["vs_out"][t * 128:(t + 1) * 128, :], scr[:], [f"scr{t}"], (), dsem=f"vs_out{t}", is_output=True)
                self.cp("act", s["vn"][:, t, :], scr[:], [f"scr{t}"], [f"vn{t}"])
            import os
            for nb in range(AW // 512 if (self.stage > 2 and os.environ.get("MK_SKIPA3") != "1") else 0):
                def ev(j, ps, pres, nb=nb):
                    h = nb * 4 + j
                    gi = self.rr["gu"] % 2
                    self.rr["gu"] += 1
                    gu = s[f"gu{gi}"]
                    self.act(gu[:], ps, AF.Gelu, pres, [f"gu{gi}"])
                    b = self.next_bank("g")
                    for t in range(NT):
                        self.mm(self.pb[b][:, t * 128:(t + 1) * 128], s["vn"][:, t, h * 128:(h + 1) * 128], s["Wt"][:, h, :], True, True,
                                [f"vn{t}", "Wt"], self.bankres(b))
                    self.tt(s["tmpf"][:].rearrange("p (t i) -> p t i", t=NT), self.pb[b][:, 0:TG].rearrange("p (t i) -> p t i", t=NT),
                            s["bsbc"][:, h, :].unsqueeze(1).to_broadcast([128, NT, 128]), ALU.add, self.bankres(b) + ["bsbc"], ["tmpf"])
                    self.tt(s["catT"][:, h, :], s["tmpf"][:], gu[:], ALU.mult, ["tmpf", f"gu{gi}"], ["catT"])
                self.fm_block(srcT_fm, ["hT"], KD, d["w_in"], 0, [(nb * 512, 512)], ev)

        if self.stage <= 3:
            return
        import os
        a4 = int(os.environ.get("MK_A4", "15"))
        for hh in range(BH):
            if full and (a4 & 1):
                def evq(j, ps, pres):
                    if j >= KC:
                        self.cp("act", s["tmpf"][:], ps, pres, ["tmpf"])
                        return
                    self.P.op("act", lambda e, o=s["qT"][:, j, :], i=ps: e.mul(out=o, in_=i, mul=float(DK) ** -0.5), pres, ["qT"])

                def evkt(j, ps, pres):
                    self.cp(os.environ.get("MK_KT", "dve"), s["kT"][:, j, :], ps, pres, ["kT"])
                cb = (hh * DK // 512) * 512
                chs = [((hh * DK) % 512) // 128 + i for i in range(KC)]
                qk = int(os.environ.get("MK_QK", "3"))
                if qk & 1:
                    self.fm_block(srcT_fm, ["hT"], KD, d["w_in"], 0, [(c.QOFF + cb, 512)], evq, chunks=(chs if os.environ.get("MK_CH", "1") == "1" else None))
                if qk & 2:
                    self.fm_block(srcT_fm, ["hT"], KD, d["w_in"], 0, [(c.KOFF + cb, 512)], evkt, chunks=chs)

            def evk(t, ps, pres):
                self.cp("dve", s["ktm"][:, t, :], ps, pres, [f"ktm{t}"])
            if a4 & 2:
                self.tm_block(srcT_tm, ["hT"], KD, d["w_in"], 0, [(c.KOFF + (hh * DK // 512) * 512, 512)], evk, use_cols=((hh * DK) % 512, DK))

            def evv(t, ps, pres):
                self.cp("act", s["vtm"][:, t, :], ps, pres, [f"vtm{t}"])
            if a4 & 4:
                self.tm_block(srcT_tm, ["hT"], KD, d["w_in"], 0, [(c.VOFF + hh * DV, DV)], evv)
            if full and (a4 & 8):
                def evr(t, ps, pres):
                    self.act(s["rs"][:, t, :], ps, AF.Silu, pres, [f"rs{t}"])
                self.tm_block(srcT_tm, ["hT"], KD, d["w_in"], 0, [(c.ROFF + hh * DV, DV)], evr)
            for t in range(NT):
                self.gla_tile(hh, t, mode)

        if not full:
            return
        if self.stage <= 4:
            return
        catT = s["catT"]
        for nb in range(D // 512):
            def evo(t, ps, pres, nb=nb):
                self.tt(s["x1"][:, t, nb * 512:(nb + 1) * 512], s["x1"][:, t, nb * 512:(nb + 1) * 512], ps, ALU.add, pres + [f"x1_{t}"], [f"x1_{t}"])
            self.tm_block(lambda k, t: catT[:, k, t * 128:(t + 1) * 128], ["catT"], KD, d["w_out"], 0, [(nb * 512, 512)], evo)
        for t in range(NT):
            self.norm_T(t, s["gffnc"], "gffnc", hT, "hT")
        if self.stage <= 5:
            return
        blocks = []
        f0 = 0
        while f0 < FF:
            fw = min(512, FF - f0)
            blocks.append((f0, fw))
            f0 += fw
        pending = None

        def do_down(f0, fw, ai):
            actT = s[f"actT{ai}"]
            nj = fw // 128
            wcols = min(4096 // nj, D)
            for cq in range(0, D, wcols):
                wv, wres = self.wload(d["w_fd"], f0, nj, [(cq, wcols)])
                for cb in range(0, wcols, 512):
                    for t in range(NT):
                        b = 4 + self.rr["g"] % 4
                        self.rr["g"] += 1
                        for j in range(nj):
                            self.mm(self.pb[b][:, 0:512], actT[:, j, t * 128:(t + 1) * 128], wv[:, j, cb:cb + 512], j == 0, j == nj - 1,
                                    [f"actT{ai}"] + wres, self.bankres(b))
                        col = cq + cb
                        self.tt(s["x1"][:, t, col:col + 512], s["x1"][:, t, col:col + 512], self.pb[b][:, 0:512], ALU.add,
                                self.bankres(b) + [f"x1_{t}"], [f"x1_{t}"])

        for (f0, fw) in blocks:
            ai = self.rr["actT"] % 2
            self.rr["actT"] += 1
            actT = s[f"actT{ai}"]
            gate_ps = {}
            if fw == 512:
                lc0, chs = f0, [0, 1, 2, 3]
            else:
                lc0 = FF - 512
                chs = list(range((f0 - lc0) // 128, 4))

            def evg(j, ps, pres):
                gate_ps[j] = (ps, pres)
            self.fm_block(srcT_fm, ["hT"], KD, d["w_fg"], 0, [(lc0, 512)], evg, chunks=chs)

            def evu(j, ps, pres, ai=ai, actT=actT):
                gps, gres = gate_ps[j]
                gi = self.rr["gu"] % 2
                self.rr["gu"] += 1
                gu = s[f"gu{gi}"]
                self.act(gu[:], gps, AF.Silu, gres, [f"gu{gi}"])
                self.tt(actT[:, j, :], gu[:], ps, ALU.mult, pres + [f"gu{gi}"], [f"actT{ai}"])
            self.fm_block(srcT_fm, ["hT"], KD, d["w_fu"], 0, [(lc0, 512)], evu, chunks=chs)
            if pending is not None:
                do_down(*pending)
            pending = (f0, fw, ai)
        do_down(*pending)
        self.dma("sp", s["gbc"][:], d["gfin"].partition_broadcast(128), (), ["gbc", "gbc2"], dsem="gbc")
        for t in range(NT):
            for hf in range(2):
                self.act(s[f"scr{hf}"][:], s["x1"][:, t, hf * AW:(hf + 1) * AW], AF.Square, [f"x1_{t}"], [f"scr{hf}", f"st{hf}"],
                         scale=1.0 / math.sqrt(D), accum_out=st[:, hf:hf + 1])
            self.tt(st[:, 2:3], st[:, 0:1], st[:, 1:2], ALU.add, ["st0", "st1"], ["st2"])
            self.act(st[:, 3:4], st[:, 2:3], AF.Sqrt, ["st2"], ["st3"], bias=EPS)
            self.recip(st[:, 4:5], st[:, 3:4], ["st3"], ["st4"])
            self.stt(s["x1"][:, t, :], s["x1"][:, t, :], st[:, 4:5], s["gbc"][:], ALU.mult, ALU.mult, [f"x1_{t}", "st4", "gbc", "gbc2"], [f"x1_{t}"])
            self.dma("sp", ydst[t * 128:(t + 1) * 128, :], s["x1"][:, t, :], [f"x1_{t}"], (), dsem=f"x1_{t}", is_output=True)

    def gla_tile(self, hh, t, mode):
        c, s, d = self.c, self.s, self.d
        DK, DV, KC, NE = c.DK, c.DV, c.KC, c.NE
        full = mode != "pre"
        sample = mode == "sample"
        st = s["stats"]
        Sres = f"S{hh}"
        sph = s["spg"][:, t, hh * DK:(hh + 1) * DK]
        tsl = slice(t * 128, (t + 1) * 128)
        import os
        if int(os.environ.get("MK_GLA", "99")) <= 0:
            return
        for kc in range(KC):
            b = self.next_bank("g")
            self.mm(self.pb[b][:, 0:NE], sph[:, kc * 128:(kc + 1) * 128], s["ucat"][:], True, True, [f"spg{t}", "ucat"], self.bankres(b))
            self.act(s["E"][:, kc, :], self.pb[b][:, 0:NE], AF.Exp, self.bankres(b), [f"E{kc}"])
        b = self.next_bank("g")
        self.mm(self.pb[b][:, 0:DK], s["u3"][:], sph, True, True, [f"spg{t}", "u3"], self.bankres(b))
        self.act(s["EB3"][:], self.pb[b][:, 0:DK], AF.Exp, self.bankres(b), ["EB3"])
        self.tt(s["kbAB"][0:64, 0, :], s["ktm"][0:64, t, :], s["EB3"][0:64, :], ALU.mult, [f"ktm{t}", "EB3"], ["kbAB"])
        self.tt(s["kbAB"][64:128, 1, :], s["ktm"][64:128, t, :], s["EB3"][64:128, :], ALU.mult, [f"ktm{t}", "EB3"], ["kbAB"])
        import os
        gst = int(os.environ.get("MK_GLA", "99"))
        if gst <= 1:
            return
        if full:
            for kc in range(KC):
                E = s["E"]
                self.tt(s["qeT"][:, kc, :], s["qT"][:, kc, tsl], E[:, kc, 0:128], ALU.mult, ["qT", f"E{kc}"], ["qeT"])
                self.tt(s["keT"][:, kc, :], s["kT"][:, kc, tsl], E[:, kc, 128:256], ALU.mult, ["kT", f"E{kc}"], ["keT"])
                self.tt(s["qbAB"][:, kc, 0, 0:64], s["qT"][:, kc, t * 128:t * 128 + 64], E[:, kc, 256:320], ALU.mult, ["qT", f"E{kc}"], ["qbAB"])
                self.tt(s["qbAB"][:, kc, 1, 64:128], s["qT"][:, kc, t * 128 + 64:t * 128 + 128], E[:, kc, 320:384], ALU.mult, ["qT", f"E{kc}"], ["qbAB"])
            b = self.next_bank("g")
            for kc in range(KC):
                self.mm(self.pb[b][:, 0:128], s["keT"][:, kc, :], s["qeT"][:, kc, :], kc == 0, kc == KC - 1, ["keT", "qeT"], self.bankres(b))
            self.tt(s["attT"][:], self.pb[b][:, 0:128], s["maskS"][:], ALU.mult, self.bankres(b) + ["maskS"], ["attT"])
            po = self.next_bank("tp")
            pores = self.bankres(po)
        if gst <= 2:
            return
        for blk in range(2):
            seq = 2 * t + blk
            if sample:
                self.dma("sp", s["S"][:, hh], d["st"][seq, hh].rearrange("(k p) e -> p k e", p=128), (), [Sres], dsem=Sres)
            if full and (sample or (blk == 0 and t == 0)):
                self.cp("act", s["Sbf"][:], s["S"][:, hh], [Sres], ["Sbf"])
            if full:
                if blk == 0:
                    self.mm(self.pb[po][:, 0:DV], s["attT"][:], s["vtm"][:, t, :], True, False, ["attT", f"vtm{t}"], pores)
                for kc in range(KC):
                    self.mm(self.pb[po][:, 0:DV], s["qbAB"][:, kc, blk, :], s["Sbf"][:, kc, :], False, (blk == 1 and kc == KC - 1),
                            ["qbAB", "Sbf"], pores)
            for kc in range(KC):
                b = self.next_bank("g")
                self.mm(self.pb[b][:, 0:DV], s["kbAB"][:, blk, kc * 128:(kc + 1) * 128], s["vtm"][:, t, :], True, True,
                        ["kbAB", f"vtm{t}"], self.bankres(b))
                self.stt(s["S"][:, hh, kc, :], s["S"][:, hh, kc, :], s["E"][:, kc, 384 + blk:385 + blk], self.pb[b][:, 0:DV], ALU.mult, ALU.add,
                         [Sres, f"E{kc}"] + self.bankres(b), [Sres])
            if sample:
                self.dma("sp", d["ss_out"][seq, hh].rearrange("(k p) e -> p k e", p=128), s["S"][:, hh], [Sres], (), dsem=Sres, is_output=True)
            elif full and not (blk == 1 and t == c.NT - 1):
                self.cp("act", s["Sbf"][:], s["S"][:, hh], [Sres], ["Sbf"])
        if not full:
            return
        if gst <= 3:
            return
        self.act(s["otmp"][:], self.pb[po][:, 0:DV], AF.Square, pores, ["otmp", "st5"], scale=1.0 / math.sqrt(DV), accum_out=st[:, 5:6])
        self.act(st[:, 6:7], st[:, 5:6], AF.Sqrt, ["st5"], ["st6"], bias=EPS)
        self.recip(st[:, 7:8], st[:, 6:7], ["st6"], ["st7"])
        self.stt(s["otmp"][:], self.pb[po][:, 0:DV], st[:, 7:8], s["gnbc"][:], ALU.mult, ALU.mult, pores + ["st7", "gnbc", "otmp"], ["otmp"])
        self.tt(s["bout"][:], s["otmp"][:], s["rs"][:, t, :], ALU.mult, ["otmp", f"rs{t}"], ["bout"])
        b = self.next_bank("tp")
        pbf = self.pb[b][:].bitcast(BF16)
        nq = DV // 128
        for j in range(nq):
            self.tr(pbf[:, j * 128:(j + 1) * 128], s["bout"][:, j * 128:(j + 1) * 128], s["identb"][:], ["bout", "identb"], self.bankres(b))
        c0 = c.AH + hh * nq
        self.cp("act", s["catT"][:, c0:c0 + nq, tsl], pbf[:, 0:nq * 128].rearrange("p (a b) -> p a b", a=nq), self.bankres(b), ["catT"])


_NC_CACHE = {}


def make_in_maps(cfg, inputs):
    c = cfg
    f = lambda a: np.ascontiguousarray(np.asarray(a, dtype=np.float32))
    xp = f(inputs["x_prompt"])[0]
    xs = f(inputs["x_sample"])
    stt = f(inputs["state_gla"])[0]
    w_s = f(inputs["w_s"])[0]
    b_s = f(inputs["b_s"])[0]
    AH = c.AH
    wsT_p = np.ascontiguousarray(w_s.transpose(2, 0, 1).reshape(128, AH * 128))
    blk = np.zeros((AH, 128, 128), np.float32)
    blk[:, :64, :64] = w_s[:, :64, :64]
    blk[:, 64:, 64:] = w_s[:, :64, :64]
    wsT_s = np.ascontiguousarray(blk.transpose(2, 0, 1).reshape(128, AH * 128))
    bs_p = np.ascontiguousarray(b_s.reshape(1, AH * 128))
    bs_s = np.ascontiguousarray(np.concatenate([b_s[:, :64], b_s[:, :64]], axis=1).reshape(1, AH * 128))
    shared = dict(
        w_in=f(inputs["w_in"])[0], w_out=f(inputs["w_out"])[0], w_fg=f(inputs["w_ffn_gate"])[0], w_fu=f(inputs["w_ffn_up"])[0],
        w_fd=f(inputs["w_ffn_down"])[0],
        gmixc=np.ascontiguousarray(f(inputs["g_mix"])[0].reshape(c.KD, 128).T), gffnc=np.ascontiguousarray(f(inputs["g_ffn"])[0].reshape(c.KD, 128).T),
        gfin=f(inputs["g_final"]).reshape(1, c.D), ln_g=f(inputs["ln_g"]), ln_b=f(inputs["ln_b"]),
        wgu=np.ascontiguousarray(np.concatenate([f(inputs["w_gate_up"])[0], f(inputs["b_gate"])], axis=0)),
        gnorm=f(inputs["gla_norm_g"]), wsT_p=wsT_p, wsT_s=wsT_s, bs_p=bs_p, bs_s=bs_s, **host_consts(c))
    ntok = c.NPG * c.TG
    pre_tok = max(c.LBG, 1) * c.TG
    in_maps = []
    for core in range(c.NCORES):
        m = dict(shared)
        m["xp"] = np.ascontiguousarray(xp[core * ntok:(core + 1) * ntok])
        m["xs"] = np.ascontiguousarray(xs[core * c.NSEQ:(core + 1) * c.NSEQ].reshape(c.TG, c.D))
        m["st"] = np.ascontiguousarray(stt[core * c.NSEQ:(core + 1) * c.NSEQ])
        pre = np.zeros((pre_tok, c.D), np.float32)
        avail = min(core * ntok, c.LBG * c.TG)
        if avail > 0:
            pre[pre_tok - avail:] = xp[core * ntok - avail:core * ntok]
        m["xpre"] = pre
        in_maps.append(m)
    return in_maps


def run_cfg(cfg, inputs, key="main"):
    if key not in _NC_CACHE:
        _NC_CACHE[key] = MK(cfg).build()
    nc = _NC_CACHE[key]
    in_maps = make_in_maps(cfg, inputs)
    res = run_bass_kernel_spmd(nc, in_maps, core_ids=list(range(cfg.NCORES)))
    r = res.results
    c = cfg
    y_prompt = np.concatenate([r[i]["yp"] for i in range(c.NCORES)], axis=0)[None]
    y_sample = np.concatenate([r[i]["ys"].reshape(c.NSEQ, 64, c.D) for i in range(c.NCORES)], axis=0)
    new_gla_prompt = r[c.NCORES - 1]["sp_out"][None, None]
    new_gla_sample = np.concatenate([r[i]["ss_out"] for i in range(c.NCORES)], axis=0)[None]
    new_sgu = np.concatenate([r[i]["vs_out"].reshape(c.NSEQ, 64, c.AW) for i in range(c.NCORES)], axis=0)[None]
    return (y_prompt.astype(np.float32), y_sample.astype(np.float32), new_gla_prompt.astype(np.float32),
            new_gla_sample.astype(np.float32), new_sgu.astype(np.float32))


def kernel(**inputs):
    cfg = Cfg(D=4096, AH=16, BH=4, NPG=8, NSEQ=4, LBG=8, NCORES=8)
    return run_cfg(cfg, inputs)
```

```python
import math
from contextlib import ExitStack

import numpy as np
import concourse.bass as bass
import concourse.mybir as mybir
from concourse.bass_utils import run_bass_kernel_spmd

F32 = mybir.dt.float32
BF16 = mybir.dt.bfloat16
AF = mybir.ActivationFunctionType
ALU = mybir.AluOpType

ENGS = ("pe", "act", "dve", "pool", "sp")
EPS = 1e-6
GATE_TAU = 16.0


class Op:
    __slots__ = ("eng", "pos", "fn", "waits", "sig", "val", "dsem", "dval")

    def __init__(self, eng, pos, fn):
        self.eng, self.pos, self.fn = eng, pos, fn
        self.waits = []
        self.sig = False
        self.val = None
        self.dsem = None
        self.dval = None


class Prog:
    def __init__(self, same_engine_sync=True):
        self.ops = {e: [] for e in ENGS}
        self.res = {}
        self.seen = {e: {} for e in ENGS}
        self.dcount = {}
        self.same_engine_sync = same_engine_sync
        self.out_dsems = set()

    def _dep(self, op, prod):
        if prod is None or prod is op:
            return
        if prod.dsem is not None:
            key, v = ("d", prod.dsem), prod.dval
        else:
            if prod.eng == op.eng and (prod.eng == "pe" or not self.same_engine_sync):
                return
            key, v = ("e", prod.eng), prod.pos
        seen = self.seen[op.eng]
        if seen.get(key, -1) >= v:
            return
        seen[key] = v
        if prod.dsem is not None:
            op.waits.append((prod.dsem, prod.dval))
        else:
            prod.sig = True
            op.waits.append(prod)

    def op(self, eng, fn, reads=(), writes=(), dsem=None, is_output=False):
        lst = self.ops[eng]
        o = Op(eng, len(lst), fn)
        if dsem is not None:
            c = self.dcount.get(dsem, 0) + 1
            self.dcount[dsem] = c
            o.dsem, o.dval = dsem, c
            if is_output:
                self.out_dsems.add(dsem)
        prods = []
        for r in reads:
            st = self.res.setdefault(r, [None, []])
            prods.append(st[0])
        for w in writes:
            st = self.res.setdefault(w, [None, []])
            prods.append(st[0])
            prods.extend(st[1])
        prods = [p for p in prods if p is not None]
        prods.sort(key=lambda p: -(p.dval if p.dsem is not None else p.pos))
        for p in prods:
            self._dep(o, p)
        for r in reads:
            self.res[r][1].append(o)
        for w in writes:
            st = self.res[w]
            st[0] = o
            st[1] = []
        lst.append(o)
        return o

    def emit(self, nc, stack):
        esem = {e: stack.enter_context(nc.semaphore("es_" + e)) for e in ENGS}
        dsem = {n: stack.enter_context(nc.semaphore("ds_" + n)) for n in self.dcount}
        for e in ENGS:
            c = 0
            for o in self.ops[e]:
                if o.sig and o.dsem is None:
                    c += 1
                    o.val = c
        final_waits = [(n, self.dcount[n]) for n in sorted(self.out_dsems)]
        block = stack.enter_context(nc.Block())

        def run(engname):
            def body(eng):
                for o in self.ops[engname]:
                    for w in o.waits:
                        if isinstance(w, tuple):
                            eng.wait_ge(dsem[w[0]], 16 * w[1])
                        else:
                            eng.wait_ge(esem[w.eng], w.val)
                    ins = o.fn(eng)
                    if o.dsem is not None:
                        ins.then_inc(dsem[o.dsem], 16)
                    elif o.sig:
                        ins.then_inc(esem[engname], 1)
                if engname == "sp":
                    for n, c in final_waits:
                        eng.wait_ge(dsem[n], 16 * c)
            return body

        block.tensor(run("pe"))
        block.scalar(run("act"))
        block.vector(run("dve"))
        block.gpsimd(run("pool"))
        block.sync(run("sp"))


class Cfg:
    def __init__(self, D=4096, AH=16, BH=4, NPG=8, NSEQ=4, LBG=56, NCORES=8):
        self.D = D
        self.AW = D // 2
        self.AH = AH
        assert self.AW // AH == 128
        self.BH = BH
        self.BW = D - self.AW
        self.BK = self.BW // 2
        self.DK = self.BK // BH
        self.DV = self.BW // BH
        assert self.DK % 128 == 0 and self.DV == 512
        self.R = 16
        self.FF = -(-8 * D // (3 * 256)) * 256
        self.IN_COLS = 2 * self.AW + 2 * self.BK + 2 * self.BW + self.R
        self.TG = 256
        self.NT = 2
        self.KD = D // 128
        self.KC = self.DK // 128
        self.NPG = NPG
        self.NSEQ = NSEQ
        assert NSEQ * 64 == self.TG
        self.LBG = LBG
        self.NCORES = NCORES
        self.QOFF = 2 * self.AW
        self.KOFF = self.QOFF + self.BK
        self.VOFF = self.KOFF + self.BK
        self.ROFF = self.VOFF + self.BW
        self.GOFF = self.ROFF + self.BW
        self.NE = 3 * 128 + 2


def host_consts(cfg):
    j = np.arange(128)[:, None]
    i = np.arange(128)[None, :]
    same = (j // 64) == (i // 64)
    ident = np.eye(128, dtype=np.float32)
    maskP = (j <= i).astype(np.float32)
    maskS = ((j <= i) & same).astype(np.float32)
    s = -1.0 / GATE_TAU
    ref = (i // 64) * 64 + 32
    U1 = (((j <= i) & same).astype(np.float32) - ((j <= ref) & same).astype(np.float32)) * s
    Ub = ((j <= i) & same).astype(np.float32) * s
    U3 = ((j > i) & same).astype(np.float32) * s
    blk = np.zeros((128, 2), np.float32)
    blk[:64, 0] = s
    blk[64:, 1] = s
    ucat = np.concatenate([U1, -U1, Ub, blk], axis=1).astype(np.float32)
    return dict(c_ident=ident, c_maskP=maskP, c_maskS=maskS, c_ucat=ucat, c_u3=U3.astype(np.float32),
                c_ones=np.ones((1, cfg.TG), np.float32))


class MK:
    def __init__(self, cfg):
        self.c = cfg

    def build(self):
        c = self.c
        nc = bass.Bass("TRN2", target_bir_lowering=False)
        self.nc = nc
        D, AW, AH, BH, DK, DV, FF, TG, NT, KD, KC = c.D, c.AW, c.AH, c.BH, c.DK, c.DV, c.FF, c.TG, c.NT, c.KD, c.KC

        def din(name, shape):
            return nc.dram_tensor(name, list(shape), F32, kind="ExternalInput").ap()

        def dout(name, shape):
            return nc.dram_tensor(name, list(shape), F32, kind="ExternalOutput").ap()

        d = self.d = {}
        d["xp"] = din("xp", [c.NPG * TG, D])
        d["xs"] = din("xs", [TG, D])
        d["xpre"] = din("xpre", [max(c.LBG, 1) * TG, D])
        d["st"] = din("st", [c.NSEQ, BH, DK, DV])
        d["w_in"] = din("w_in", [D, c.IN_COLS])
        d["w_out"] = din("w_out", [D, D])
        d["w_fg"] = din("w_fg", [D, FF])
        d["w_fu"] = din("w_fu", [D, FF])
        d["w_fd"] = din("w_fd", [FF, D])
        d["gmixc"] = din("gmixc", [128, KD])
        d["gffnc"] = din("gffnc", [128, KD])
        d["gfin"] = din("gfin", [1, D])
        d["ln_g"] = din("ln_g", [1, AW])
        d["ln_b"] = din("ln_b", [1, AW])
        d["wgu"] = din("wgu", [17, BH * DK])
        d["gnorm"] = din("gnorm", [1, DV])
        d["wsT_p"] = din("wsT_p", [128, AH * 128])
        d["wsT_s"] = din("wsT_s", [128, AH * 128])
        d["bs_p"] = din("bs_p", [1, AH * 128])
        d["bs_s"] = din("bs_s", [1, AH * 128])
        d["c_ident"] = din("c_ident", [128, 128])
        d["c_maskP"] = din("c_maskP", [128, 128])
        d["c_maskS"] = din("c_maskS", [128, 128])
        d["c_ucat"] = din("c_ucat", [128, c.NE])
        d["c_u3"] = din("c_u3", [128, 128])
        d["c_ones"] = din("c_ones", [1, TG])
        d["yp"] = dout("yp", [c.NPG * TG, D])
        d["ys"] = dout("ys", [TG, D])
        d["sp_out"] = dout("sp_out", [BH, DK, DV])
        d["ss_out"] = dout("ss_out", [c.NSEQ, BH, DK, DV])
        d["vs_out"] = dout("vs_out", [TG, AW])

        self.wb = {}
        for nm in ["w_in", "w_out", "w_fg", "w_fu", "w_fd"]:
            shp = list(d[nm].shape)
            self.wb[nm] = nc.dram_tensor(nm + "_b", shp, BF16, kind="Internal").ap()
        self.P = Prog()
        with ExitStack() as st:
            self.st = st
            self.alloc()
            self.program()
            self.P.emit(nc, st)
        return nc

    def sb(self, name, shape, dt):
        return self.st.enter_context(self.nc.sbuf_tensor("s_" + name, list(shape), dt))

    def alloc(self):
        c = self.c
        D, AW, AH, BH, DK, DV, TG, NT, KD, KC = c.D, c.AW, c.AH, c.BH, c.DK, c.DV, c.TG, c.NT, c.KD, c.KC
        s = self.s = {}
        s["x1"] = self.sb("x1", [128, NT, D], F32)
        s["hT"] = self.sb("hT", [128, KD, TG], BF16)
        s["catT"] = self.sb("catT", [128, KD, TG], BF16)
        s["scr0"] = self.sb("scr0", [128, AW], F32)
        s["scr1"] = self.sb("scr1", [128, AW], F32)
        s["vn"] = self.sb("vn", [128, NT, AW], BF16)
        s["spg"] = self.sb("spg", [128, NT, BH * DK], F32)
        s["glrT"] = self.sb("glrT", [17, TG], F32)
        s["qT"] = self.sb("qT", [128, KC, TG], F32)
        s["kT"] = self.sb("kT", [128, KC, TG], F32)
        s["ktm"] = self.sb("ktm", [128, NT, DK], F32)
        s["vtm"] = self.sb("vtm", [128, NT, DV], BF16)
        s["rs"] = self.sb("rs", [128, NT, DV], BF16)
        s["E"] = self.sb("E", [128, KC, c.NE], F32)
        s["EB3"] = self.sb("EB3", [128, DK], F32)
        s["qeT"] = self.sb("qeT", [128, KC, 128], BF16)
        s["keT"] = self.sb("keT", [128, KC, 128], BF16)
        s["qbAB"] = self.sb("qbAB", [128, KC, 2, 128], BF16)
        s["kbAB"] = self.sb("kbAB", [128, 2, DK], BF16)
        s["attT"] = self.sb("attT", [128, 128], BF16)
        s["otmp"] = self.sb("otmp", [128, DV], F32)
        s["bout"] = self.sb("bout", [128, DV], BF16)
        s["stats"] = self.sb("stats", [128, 32], F32)
        s["S"] = self.sb("S", [128, BH, KC, DV], F32)
        s["Sbf"] = self.sb("Sbf", [128, KC, DV], BF16)
        s["actT0"] = self.sb("actT0", [128, 4, TG], BF16)
        s["actT1"] = self.sb("actT1", [128, 4, TG], BF16)
        s["gu0"] = self.sb("gu0", [128, TG], F32)
        s["gu1"] = self.sb("gu1", [128, TG], F32)
        s["tmpf"] = self.sb("tmpf", [128, TG], F32)
        s["gbc"] = self.sb("gbc", [128, D], F32)
        self.NW = 3
        for i in range(self.NW):
            s[f"w{i}"] = self.sb(f"w{i}", [128, 4096], BF16)
        s["ident"] = self.sb("ident", [128, 128], F32)
        s["identb"] = self.sb("identb", [128, 128], BF16)
        s["maskP"] = self.sb("maskP", [128, 128], F32)
        s["maskS"] = self.sb("maskS", [128, 128], F32)
        s["ucat"] = self.sb("ucat", [128, c.NE], F32)
        s["u3"] = self.sb("u3", [128, 128], F32)
        s["Wt"] = self.sb("Wt", [128, AH, 128], BF16)
        s["bsbc"] = self.sb("bsbc", [128, AH, 128], F32)
        s["gmixc"] = self.sb("gmixc", [128, KD], F32)
        s["gffnc"] = self.sb("gffnc", [128, KD], F32)
        s["wg"] = self.sb("wg", [128, KD, 16], BF16)
        s["wgu"] = self.sb("wgu", [17, BH * DK], F32)
        s["gnbc"] = self.sb("gnbc", [128, DV], F32)
        self.pb = [self.st.enter_context(self.nc.psum_tensor(f"pb{i}", [128, 512], F32)) for i in range(8)]
        self.rr = {"acc": 0, "tp": 0, "g": 0, "w": 0, "gu": 0, "actT": 0}

    def bankres(self, i):
        return [f"pb{i}a", f"pb{i}b"]

    def next_bank(self, pool):
        rng = {"acc": (0, 4), "tp": (4, 2), "g": (6, 2)}[pool]
        i = rng[0] + self.rr[pool] % rng[1]
        self.rr[pool] += 1
        return i

    def mm(self, out, lhsT, rhs, start, stop, reads, writes):
        self.P.op("pe", lambda e: e.matmul(out, lhsT=lhsT, rhs=rhs, start=start, stop=stop), reads, writes)

    def tr(self, out, in_, ident, reads, writes):
        self.P.op("pe", lambda e: e.transpose(out=out, in_=in_, identity=ident), reads, writes)

    def act(self, out, in_, func, reads, writes, **kw):
        self.P.op("act", lambda e: e.activation(out=out, in_=in_, func=func, **kw), reads, writes)

    def cp(self, eng, out, in_, reads, writes):
        if eng == "act":
            self.P.op("act", lambda e: e.copy(out=out, in_=in_), reads, writes)
        else:
            self.P.op(eng, lambda e: e.tensor_copy(out=out, in_=in_), reads, writes)

    def tt(self, out, in0, in1, op, reads, writes):
        self.P.op("dve", lambda e: e.tensor_tensor(out=out, in0=in0, in1=in1, op=op), reads, writes)

    def ts(self, out, in0, s1, s2, op0, op1, reads, writes):
        self.P.op("dve", lambda e: e.tensor_scalar(out=out, in0=in0, scalar1=s1, scalar2=s2, op0=op0, op1=op1), reads, writes)

    def stt(self, out, in0, scalar, in1, op0, op1, reads, writes):
        self.P.op("dve", lambda e: e.scalar_tensor_tensor(out=out, in0=in0, scalar=scalar, in1=in1, op0=op0, op1=op1), reads, writes)

    def recip(self, out, in_, reads, writes):
        self.P.op("dve", lambda e: e.reciprocal(out=out, in_=in_), reads, writes)

    def dma(self, q, out, in_, reads, writes, dsem, is_output=False):
        return self.P.op(q, lambda e: e.dma_start(out=out, in_=in_), reads, writes, dsem=dsem, is_output=is_output)

    def memset(self, eng, ap, val, writes):
        self.P.op(eng, lambda e: e.memset(ap, val), (), writes)

    def wload(self, wd, row0, nkc, colranges):
        ncols = sum(n for _, n in colranges)
        assert nkc * ncols <= 4096
        si = self.rr["w"] % self.NW
        self.rr["w"] += 1
        buf = self.s[f"w{si}"]
        view = buf[:, 0:nkc * ncols].rearrange("p (k n) -> p k n", k=nkc)
        off = 0
        assert len(colranges) <= 2
        for idx, (c0, n) in enumerate(colranges):
            src = self.wb[wd][row0:row0 + nkc * 128, c0:c0 + n].rearrange("(k p) n -> p k n", p=128)
            if len(colranges) == 1:
                wr = [f"w{si}", f"w{si}b"]
            else:
                wr = [f"w{si}"] if idx == 0 else [f"w{si}b"]
            self.dma("pool", view[:, :, off:off + n], src, reads=["cv_" + wd], writes=wr, dsem=f"w{si}")
            off += n
        return view, [f"w{si}", f"w{si}b"]

    @staticmethod
    def _offs(colranges):
        out, off = [], 0
        for (_, n) in colranges:
            if off:
                out.append(off)
            off += n
        return out

    def tm_block(self, src, src_res, nk_total, wd, row0, colranges, evac, unit_k=8, use_cols=None):
        c = self.c
        ncols = sum(n for _, n in colranges)
        uoff, ucols = use_cols if use_cols else (0, ncols)
        banks = [self.next_bank("acc") for _ in range(c.NT)]
        for u0 in range(0, nk_total, unit_k):
            nk = min(unit_k, nk_total - u0)
            wv, wres = self.wload(wd, row0 + u0 * 128, nk, colranges)
            for t in range(c.NT):
                for kc in range(nk):
                    k = u0 + kc
                    self.mm(self.pb[banks[t]][:, 0:ucols], src(k, t), wv[:, kc, uoff:uoff + ucols], k == 0, k == nk_total - 1,
                            reads=src_res + wres, writes=self.bankres(banks[t]))
        for t in range(c.NT):
            evac(t, self.pb[banks[t]][:, 0:ucols], self.bankres(banks[t]))

    def fm_block(self, src, src_res, nk_total, wd, row0, colranges, evac, unit_k=8, chunks=None):
        c = self.c
        ncols = sum(n for _, n in colranges)
        chunks = list(range(ncols // 128)) if chunks is None else chunks
        nj = len(chunks)
        halves = []
        for jj in range(0, nj, 2):
            b = self.next_bank("acc")
            halves.append((b, 0))
            if jj + 1 < nj:
                halves.append((b, 1))
        for u0 in range(0, nk_total, unit_k):
            nk = min(unit_k, nk_total - u0)
            wv, wres = self.wload(wd, row0 + u0 * 128, nk, colranges)
            for j in range(nj):
                b, h = halves[j]
                for kc in range(nk):
                    k = u0 + kc
                    self.mm(self.pb[b][:, h * 256:h * 256 + c.TG], wv[:, kc, chunks[j] * 128:(chunks[j] + 1) * 128], src(k), (k == 0 and h == 0), k == nk_total - 1,
                            reads=src_res + wres, writes=self.bankres(b))
        for j in range(nj):
            b, h = halves[j]
            evac(j, self.pb[b][:, h * 256:h * 256 + c.TG], self.bankres(b))

    def program(self):
        import os
        c = self.c
        self.stage = int(os.environ.get("MK_STAGE", "99"))
        self.setup()
        if self.stage <= 0:
            return
        for g in range(c.LBG):
            self.group(self.d["xpre"][g * c.TG:(g + 1) * c.TG, :], mode="pre", ydst=None)
        for g in range(c.NPG):
            self.group(self.d["xp"][g * c.TG:(g + 1) * c.TG, :], mode="prompt", ydst=self.d["yp"][g * c.TG:(g + 1) * c.TG, :])
        for hh in range(c.BH):
            self.dma("sp", self.d["sp_out"][hh].rearrange("(k p) e -> p k e", p=128), self.s["S"][:, hh], reads=[f"S{hh}"], writes=(),
                     dsem=f"sp_out{hh}", is_output=True)
        self.set_mode_consts("s")
        self.group(self.d["xs"], mode="sample", ydst=self.d["ys"])

    def setup(self):
        c, s, d = self.c, self.s, self.d
        last = None
        cres = []
        for name in ["ident", "maskP", "maskS", "ucat", "u3", "gmixc", "gffnc", "wgu"]:
            src = {"ident": "c_ident", "maskP": "c_maskP", "maskS": "c_maskS", "ucat": "c_ucat", "u3": "c_u3"}.get(name, name)
            last = self.dma("sp", s[name][:], d[src], (), [name], dsem="const")
            cres.append(name)
        last = self.dma("sp", s["gnbc"][:], d["gnorm"].partition_broadcast(128), (), ["gnbc"], dsem="const")
        cres.append("gnbc")
        last = self.dma("sp", s["glrT"][16:17, :], d["c_ones"], (), ["glrT_ones"], dsem="const")
        cres.append("glrT_ones")
        for r in cres:
            self.P.res[r][0] = last
        for q in range(0, c.KD, 8):
            nk = min(8, c.KD - q)
            src = d["w_in"][q * 128:(q + nk) * 128, c.GOFF:c.GOFF + 16].rearrange("(k p) n -> p k n", p=128)
            self.dma("pool", s["wg"][:, q:q + nk, :], src, (), [f"wg{q}"], dsem="wg")
        self.wg_res = [f"wg{q}" for q in range(0, c.KD, 8)]
        for nm in ["w_in", "w_out", "w_fg", "w_fu", "w_fd"]:
            rows = d[nm].shape[0]
            last = None
            for r0 in range(0, rows, 128):
                last = self.dma("pool", self.wb[nm][r0:r0 + 128, :], d[nm][r0:r0 + 128, :], (), [f"cv_{nm}_{r0}"], dsem="cv_" + nm)
            self.P.res["cv_" + nm] = [last, []]
        self.cp("dve", s["identb"][:], s["ident"][:], ["ident"], ["identb"])
        self.memset("dve", s["qbAB"][:], 0.0, ["qbAB"])
        self.memset("dve", s["kbAB"][:], 0.0, ["kbAB"])
        for hh in range(c.BH):
            self.memset("dve", s["S"][:, hh], 0.0, [f"S{hh}"])
        self.set_mode_consts("p")

    def set_mode_consts(self, m):
        c, s, d = self.c, self.s, self.d
        self.dma("sp", s["scr0"][:], d["wsT_" + m], (), ["scr0"], dsem="scr0")
        mask = s["maskP"] if m == "p" else s["maskS"]
        self.tt(s["Wt"][:], s["scr0"][:].rearrange("p (h i) -> p h i", h=c.AH), mask[:].unsqueeze(1).to_broadcast([128, c.AH, 128]), ALU.mult,
                ["scr0", "maskP", "maskS"], ["Wt"])
        self.dma("sp", s["bsbc"][:].rearrange("p h i -> p (h i)"), d["bs_" + m].partition_broadcast(128), (), ["bsbc"], dsem="bsbc")

    def norm_T(self, t, gcol, gres, dstT, dres):
        c, s = self.c, self.s
        D, AW = c.D, c.AW
        st = s["stats"]
        x1r = f"x1_{t}"
        for hf in range(2):
            self.act(s[f"scr{hf}"][:], s["x1"][:, t, hf * AW:(hf + 1) * AW], AF.Square, [x1r], [f"scr{hf}", f"st{hf}"],
                     scale=1.0 / math.sqrt(D), accum_out=st[:, hf:hf + 1])
        self.tt(st[:, 2:3], st[:, 0:1], st[:, 1:2], ALU.add, ["st0", "st1"], ["st2"])
        self.act(st[:, 3:4], st[:, 2:3], AF.Sqrt, ["st2"], ["st3"], bias=EPS)
        self.recip(st[:, 4:5], st[:, 3:4], ["st3"], ["st4"])
        for hf in range(2):
            scr = s[f"scr{hf}"]
            self.act(scr[:], s["x1"][:, t, hf * AW:(hf + 1) * AW], AF.Copy, [x1r, "st4"], [f"scr{hf}"], scale=st[:, 4:5])
            nch = AW // 128
            for c0 in range(0, nch, 4):
                b = self.next_bank("tp")
                for j in range(4):
                    self.tr(self.pb[b][:, j * 128:(j + 1) * 128], scr[:, (c0 + j) * 128:(c0 + j + 1) * 128], s["ident"][:],
                            [f"scr{hf}", "ident"], self.bankres(b))
                cc = hf * nch + c0
                self.tt(dstT[:, cc:cc + 4, t * 128:(t + 1) * 128], self.pb[b][:].rearrange("p (a b) -> p a b", a=4),
                        gcol[:, cc:cc + 4].unsqueeze(2).to_broadcast([128, 4, 128]), ALU.mult,
                        self.bankres(b) + [gres], [dres])

    def group(self, xsrc, mode, ydst):
        c, s, d = self.c, self.s, self.d
        D, AW, AH, BH, DK, DV, FF, TG, NT, KD, KC = c.D, c.AW, c.AH, c.BH, c.DK, c.DV, c.FF, c.TG, c.NT, c.KD, c.KC
        full = mode != "pre"
        st = s["stats"]
        hT = s["hT"]
        for t in range(NT):
            self.dma("sp", s["x1"][:, t, :], xsrc[t * 128:(t + 1) * 128, :], (), [f"x1_{t}"], dsem=f"x1_{t}")
        if full:
            self.dma("sp", s["gbc"][:, 0:AW], d["ln_g"].partition_broadcast(128), (), ["gbc"], dsem="gbc")
            self.dma("sp", s["gbc"][:, AW:2 * AW], d["ln_b"].partition_broadcast(128), (), ["gbc2"], dsem="gbc2")
        for t in range(NT):
            self.norm_T(t, s["gmixc"], "gmixc", hT, "hT")
        b = self.next_bank("g")
        for k in range(KD):
            self.mm(self.pb[b][0:16, 0:TG], s["wg"][:, k, :], hT[:, k, :], k == 0, k == KD - 1, ["hT"] + self.wg_res, self.bankres(b))
        self.cp("act", s["glrT"][0:16, :], self.pb[b][0:16, 0:TG], self.bankres(b), ["glrT"])
        for t in range(NT):
            for nb in range(0, BH * DK, 512):
                b = self.next_bank("g")
                wz = min(512, BH * DK - nb)
                self.mm(self.pb[b][:, 0:wz], s["glrT"][0:17, t * 128:(t + 1) * 128], s["wgu"][0:17, nb:nb + wz], True, True,
                        ["glrT", "glrT_ones", "wgu"], self.bankres(b))
                self.act(s["spg"][:, t, nb:nb + wz], self.pb[b][:, 0:wz], AF.Exp, self.bankres(b), [f"spg{t}"], scale=-1.0)
                self.act(s["spg"][:, t, nb:nb + wz], s["spg"][:, t, nb:nb + wz], AF.Ln, [f"spg{t}"], [f"spg{t}"], bias=1.0)

        if self.stage <= 1:
            return
        srcT_tm = lambda k, t: hT[:, k, t * 128:(t + 1) * 128]
        srcT_fm = lambda k: hT[:, k, :]

        if full:
            for nb in range(AW // 512):
                def ev(t, ps, pres, nb=nb):
                    self.act(s[f"scr{t}"][:, nb * 512:(nb + 1) * 512], ps, AF.Gelu, pres, [f"scr{t}", f"avs{t}_{nb}"],
                             accum_out=st[:, 8 + t * 4 + nb:9 + t * 4 + nb])
                self.tm_block(srcT_tm, ["hT"], KD, "w_in", 0, [(AW + nb * 512, 512)], ev)
            for t in range(NT):
                scr = s[f"scr{t}"]
                nsum = AW // 512
                base = 8 + t * 4
                acc_res = [f"avs{t}_{nb}" for nb in range(nsum)]
                self.P.op("dve", lambda e, o=st[:, 16 + t:17 + t], i=st[:, base:base + nsum]: e.reduce_sum(out=o, in_=i, axis=mybir.AxisListType.X),
                          acc_res, [f"avm{t}"])
                self.P.op("dve", lambda e, o=st[:, 18 + t:19 + t], i=st[:, 16 + t:17 + t]: e.tensor_scalar_mul(out=o, in0=i, scalar1=-1.0 / AW),
                          [f"avm{t}"], [f"avn{t}"])
                self.act(s["vn"][:, t, :], scr[:], AF.Square, [f"scr{t}", f"avn{t}"], [f"vn{t}", f"avv{t}"],
                         bias=st[:, 18 + t:19 + t], scale=1.0, accum_out=st[:, 20 + t:21 + t])
                self.act(st[:, 22 + t:23 + t], st[:, 20 + t:21 + t], AF.Sqrt, [f"avv{t}"], [f"avsd{t}"], scale=1.0 / AW, bias=EPS)
                self.recip(st[:, 24 + t:25 + t], st[:, 22 + t:23 + t], [f"avsd{t}"], [f"avr{t}"])
                self.ts(scr[:], scr[:], st[:, 18 + t:19 + t], st[:, 24 + t:25 + t], ALU.add, ALU.mult, [f"scr{t}", f"avn{t}", f"avr{t}"], [f"scr{t}"])
                self.tt(scr[:], scr[:], s["gbc"][:, 0:AW], ALU.mult, [f"scr{t}", "gbc"], [f"scr{t}"])
                self.tt(scr[:], scr[:], s["gbc"][:, AW:2 * AW], ALU.add, [f"scr{t}", "gbc2"], [f"scr{t}"])
                if mode == "sample":
                    self.dma("sp", d["vs_out"][t * 128:(t + 1) * 128, :], scr[:], [f"scr{t}"], (), dsem=f"vs_out{t}", is_output=True)
                self.cp("act", s["vn"][:, t, :], scr[:], [f"scr{t}"], [f"vn{t}"])
            import os
            for nb in range(AW // 512 if (self.stage > 2 and os.environ.get("MK_SKIPA3") != "1") else 0):
                def ev(j, ps, pres, nb=nb):
                    h = nb * 4 + j
                    gi = self.rr["gu"] % 2
                    self.rr["gu"] += 1
                    gu = s[f"gu{gi}"]
                    self.act(gu[:], ps, AF.Gelu, pres, [f"gu{gi}"])
                    b = self.next_bank("g")
                    for t in range(NT):
                        self.mm(self.pb[b][:, t * 128:(t + 1) * 128], s["vn"][:, t, h * 128:(h + 1) * 128], s["Wt"][:, h, :], True, True,
                                [f"vn{t}", "Wt"], self.bankres(b))
                    self.tt(s["tmpf"][:].rearrange("p (t i) -> p t i", t=NT), self.pb[b][:, 0:TG].rearrange("p (t i) -> p t i", t=NT),
                            s["bsbc"][:, h, :].unsqueeze(1).to_broadcast([128, NT, 128]), ALU.add, self.bankres(b) + ["bsbc"], ["tmpf"])
                    self.tt(s["catT"][:, h, :], s["tmpf"][:], gu[:], ALU.mult, ["tmpf", f"gu{gi}"], ["catT"])
                self.fm_block(srcT_fm, ["hT"], KD, "w_in", 0, [(nb * 512, 512)], ev)

        if self.stage <= 3:
            return
        import os
        a4 = int(os.environ.get("MK_A4", "15"))
        for hh in range(BH):
            if full and (a4 & 1):
                def evq(j, ps, pres):
                    if j >= KC:
                        self.cp("act", s["tmpf"][:], ps, pres, ["tmpf"])
                        return
                    self.P.op("act", lambda e, o=s["qT"][:, j, :], i=ps: e.mul(out=o, in_=i, mul=float(DK) ** -0.5), pres, ["qT"])

                def evkt(j, ps, pres):
                    self.cp(os.environ.get("MK_KT", "dve"), s["kT"][:, j, :], ps, pres, ["kT"])
                cb = (hh * DK // 512) * 512
                chs = [((hh * DK) % 512) // 128 + i for i in range(KC)]
                qk = int(os.environ.get("MK_QK", "3"))
                if qk & 1:
                    self.fm_block(srcT_fm, ["hT"], KD, "w_in", 0, [(c.QOFF + cb, 512)], evq, chunks=(chs if os.environ.get("MK_CH", "1") == "1" else None))
                if qk & 2:
                    self.fm_block(srcT_fm, ["hT"], KD, "w_in", 0, [(c.KOFF + cb, 512)], evkt, chunks=chs)

            def evk(t, ps, pres):
                self.cp("dve", s["ktm"][:, t, :], ps, pres, [f"ktm{t}"])
            if a4 & 2:
                self.tm_block(srcT_tm, ["hT"], KD, "w_in", 0, [(c.KOFF + (hh * DK // 512) * 512, 512)], evk, use_cols=((hh * DK) % 512, DK))

            def evv(t, ps, pres):
                self.cp("act", s["vtm"][:, t, :], ps, pres, [f"vtm{t}"])
            if a4 & 4:
                self.tm_block(srcT_tm, ["hT"], KD, "w_in", 0, [(c.VOFF + hh * DV, DV)], evv)
            if full and (a4 & 8):
                def evr(t, ps, pres):
                    self.act(s["rs"][:, t, :], ps, AF.Silu, pres, [f"rs{t}"])
                self.tm_block(srcT_tm, ["hT"], KD, "w_in", 0, [(c.ROFF + hh * DV, DV)], evr)
            for t in range(NT):
                self.gla_tile(hh, t, mode)

        if not full:
            return
        if self.stage <= 4:
            return
        catT = s["catT"]
        for nb in range(D // 512):
            def evo(t, ps, pres, nb=nb):
                self.tt(s["x1"][:, t, nb * 512:(nb + 1) * 512], s["x1"][:, t, nb * 512:(nb + 1) * 512], ps, ALU.add, pres + [f"x1_{t}"], [f"x1_{t}"])
            self.tm_block(lambda k, t: catT[:, k, t * 128:(t + 1) * 128], ["catT"], KD, "w_out", 0, [(nb * 512, 512)], evo)
        for t in range(NT):
            self.norm_T(t, s["gffnc"], "gffnc", hT, "hT")
        if self.stage <= 5:
            return
        blocks = []
        f0 = 0
        while f0 < FF:
            fw = min(512, FF - f0)
            blocks.append((f0, fw))
            f0 += fw
        pending = None

        def do_down(f0, fw, ai):
            actT = s[f"actT{ai}"]
            nj = fw // 128
            wcols = min(4096 // nj, D)
            for cq in range(0, D, wcols):
                wv, wres = self.wload("w_fd", f0, nj, [(cq, wcols)])
                for cb in range(0, wcols, 512):
                    for t in range(NT):
                        b = 4 + self.rr["g"] % 4
                        self.rr["g"] += 1
                        for j in range(nj):
                            self.mm(self.pb[b][:, 0:512], actT[:, j, t * 128:(t + 1) * 128], wv[:, j, cb:cb + 512], j == 0, j == nj - 1,
                                    [f"actT{ai}"] + wres, self.bankres(b))
                        col = cq + cb
                        self.tt(s["x1"][:, t, col:col + 512], s["x1"][:, t, col:col + 512], self.pb[b][:, 0:512], ALU.add,
                                self.bankres(b) + [f"x1_{t}"], [f"x1_{t}"])

        for (f0, fw) in blocks:
            ai = self.rr["actT"] % 2
            self.rr["actT"] += 1
            actT = s[f"actT{ai}"]
            gate_ps = {}
            if fw == 512:
                lc0, chs = f0, [0, 1, 2, 3]
            else:
                lc0 = FF - 512
                chs = list(range((f0 - lc0) // 128, 4))

            def evg(j, ps, pres):
                gate_ps[j] = (ps, pres)
            self.fm_block(srcT_fm, ["hT"], KD, "w_fg", 0, [(lc0, 512)], evg, chunks=chs)

            def evu(j, ps, pres, ai=ai, actT=actT):
                gps, gres = gate_ps[j]
                gi = self.rr["gu"] % 2
                self.rr["gu"] += 1
                gu = s[f"gu{gi}"]
                self.act(gu[:], gps, AF.Silu, gres, [f"gu{gi}"])
                self.tt(actT[:, j, :], gu[:], ps, ALU.mult, pres + [f"gu{gi}"], [f"actT{ai}"])
            self.fm_block(srcT_fm, ["hT"], KD, "w_fu", 0, [(lc0, 512)], evu, chunks=chs)
            if pending is not None:
                do_down(*pending)
            pending = (f0, fw, ai)
        do_down(*pending)
        self.dma("sp", s["gbc"][:], d["gfin"].partition_broadcast(128), (), ["gbc", "gbc2"], dsem="gbc")
        for t in range(NT):
            for hf in range(2):
                self.act(s[f"scr{hf}"][:], s["x1"][:, t, hf * AW:(hf + 1) * AW], AF.Square, [f"x1_{t}"], [f"scr{hf}", f"st{hf}"],
                         scale=1.0 / math.sqrt(D), accum_out=st[:, hf:hf + 1])
            self.tt(st[:, 2:3], st[:, 0:1], st[:, 1:2], ALU.add, ["st0", "st1"], ["st2"])
            self.act(st[:, 3:4], st[:, 2:3], AF.Sqrt, ["st2"], ["st3"], bias=EPS)
            self.recip(st[:, 4:5], st[:, 3:4], ["st3"], ["st4"])
            self.stt(s["x1"][:, t, :], s["x1"][:, t, :], st[:, 4:5], s["gbc"][:], ALU.mult, ALU.mult, [f"x1_{t}", "st4", "gbc", "gbc2"], [f"x1_{t}"])
            self.dma("sp", ydst[t * 128:(t + 1) * 128, :], s["x1"][:, t, :], [f"x1_{t}"], (), dsem=f"x1_{t}", is_output=True)

    def gla_tile(self, hh, t, mode):
        c, s, d = self.c, self.s, self.d
        DK, DV, KC, NE = c.DK, c.DV, c.KC, c.NE
        full = mode != "pre"
        sample = mode == "sample"
        st = s["stats"]
        Sres = f"S{hh}"
        sph = s["spg"][:, t, hh * DK:(hh + 1) * DK]
        tsl = slice(t * 128, (t + 1) * 128)
        import os
        if int(os.environ.get("MK_GLA", "99")) <= 0:
            return
        for kc in range(KC):
            b = self.next_bank("g")
            self.mm(self.pb[b][:, 0:NE], sph[:, kc * 128:(kc + 1) * 128], s["ucat"][:], True, True, [f"spg{t}", "ucat"], self.bankres(b))
            self.act(s["E"][:, kc, :], self.pb[b][:, 0:NE], AF.Exp, self.bankres(b), [f"E{kc}"])
        b = self.next_bank("g")
        self.mm(self.pb[b][:, 0:DK], s["u3"][:], sph, True, True, [f"spg{t}", "u3"], self.bankres(b))
        self.act(s["EB3"][:], self.pb[b][:, 0:DK], AF.Exp, self.bankres(b), ["EB3"])
        self.tt(s["kbAB"][0:64, 0, :], s["ktm"][0:64, t, :], s["EB3"][0:64, :], ALU.mult, [f"ktm{t}", "EB3"], ["kbAB"])
        self.tt(s["kbAB"][64:128, 1, :], s["ktm"][64:128, t, :], s["EB3"][64:128, :], ALU.mult, [f"ktm{t}", "EB3"], ["kbAB"])
        import os
        gst = int(os.environ.get("MK_GLA", "99"))
        if gst <= 1:
            return
        if full:
            for kc in range(KC):
                E = s["E"]
                self.tt(s["qeT"][:, kc, :], s["qT"][:, kc, tsl], E[:, kc, 0:128], ALU.mult, ["qT", f"E{kc}"], ["qeT"])
                self.tt(s["keT"][:, kc, :], s["kT"][:, kc, tsl], E[:, kc, 128:256], ALU.mult, ["kT", f"E{kc}"], ["keT"])
                self.tt(s["qbAB"][:, kc, 0, 0:64], s["qT"][:, kc, t * 128:t * 128 + 64], E[:, kc, 256:320], ALU.mult, ["qT", f"E{kc}"], ["qbAB"])
                self.tt(s["qbAB"][:, kc, 1, 64:128], s["qT"][:, kc, t * 128 + 64:t * 128 + 128], E[:, kc, 320:384], ALU.mult, ["qT", f"E{kc}"], ["qbAB"])
            b = self.next_bank("g")
            for kc in range(KC):
                self.mm(self.pb[b][:, 0:128], s["keT"][:, kc, :], s["qeT"][:, kc, :], kc == 0, kc == KC - 1, ["keT", "qeT"], self.bankres(b))
            self.tt(s["attT"][:], self.pb[b][:, 0:128], s["maskS"][:], ALU.mult, self.bankres(b) + ["maskS"], ["attT"])
            po = self.next_bank("tp")
            pores = self.bankres(po)
        if gst <= 2:
            return
        for blk in range(2):
            seq = 2 * t + blk
            if sample:
                self.dma("sp", s["S"][:, hh], d["st"][seq, hh].rearrange("(k p) e -> p k e", p=128), (), [Sres], dsem=Sres)
            if full and (sample or (blk == 0 and t == 0)):
                self.cp("act", s["Sbf"][:], s["S"][:, hh], [Sres], ["Sbf"])
            if full:
                if blk == 0:
                    self.mm(self.pb[po][:, 0:DV], s["attT"][:], s["vtm"][:, t, :], True, False, ["attT", f"vtm{t}"], pores)
                for kc in range(KC):
                    self.mm(self.pb[po][:, 0:DV], s["qbAB"][:, kc, blk, :], s["Sbf"][:, kc, :], False, (blk == 1 and kc == KC - 1),
                            ["qbAB", "Sbf"], pores)
            for kc in range(KC):
                b = self.next_bank("g")
                self.mm(self.pb[b][:, 0:DV], s["kbAB"][:, blk, kc * 128:(kc + 1) * 128], s["vtm"][:, t, :], True, True,
                        ["kbAB", f"vtm{t}"], self.bankres(b))
                self.stt(s["S"][:, hh, kc, :], s["S"][:, hh, kc, :], s["E"][:, kc, 384 + blk:385 + blk], self.pb[b][:, 0:DV], ALU.mult, ALU.add,
                         [Sres, f"E{kc}"] + self.bankres(b), [Sres])
            if sample:
                self.dma("sp", d["ss_out"][seq, hh].rearrange("(k p) e -> p k e", p=128), s["S"][:, hh], [Sres], (), dsem=Sres, is_output=True)
            elif full and not (blk == 1 and t == c.NT - 1):
                self.cp("act", s["Sbf"][:], s["S"][:, hh], [Sres], ["Sbf"])
        if not full:
            return
        if gst <= 3:
            return
        self.act(s["otmp"][:], self.pb[po][:, 0:DV], AF.Square, pores, ["otmp", "st5"], scale=1.0 / math.sqrt(DV), accum_out=st[:, 5:6])
        self.act(st[:, 6:7], st[:, 5:6], AF.Sqrt, ["st5"], ["st6"], bias=EPS)
        self.recip(st[:, 7:8], st[:, 6:7], ["st6"], ["st7"])
        self.stt(s["otmp"][:], self.pb[po][:, 0:DV], st[:, 7:8], s["gnbc"][:], ALU.mult, ALU.mult, pores + ["st7", "gnbc", "otmp"], ["otmp"])
        self.tt(s["bout"][:], s["otmp"][:], s["rs"][:, t, :], ALU.mult, ["otmp", f"rs{t}"], ["bout"])
        b = self.next_bank("tp")
        pbf = self.pb[b][:].bitcast(BF16)
        nq = DV // 128
        for j in range(nq):
            self.tr(pbf[:, j * 128:(j + 1) * 128], s["bout"][:, j * 128:(j + 1) * 128], s["identb"][:], ["bout", "identb"], self.bankres(b))
        c0 = c.AH + hh * nq
        self.cp("act", s["catT"][:, c0:c0 + nq, tsl], pbf[:, 0:nq * 128].rearrange("p (a b) -> p a b", a=nq), self.bankres(b), ["catT"])


_NC_CACHE = {}


def make_in_maps(cfg, inputs):
    c = cfg
    f = lambda a: np.ascontiguousarray(np.asarray(a, dtype=np.float32))
    xp = f(inputs["x_prompt"])[0]
    xs = f(inputs["x_sample"])
    stt = f(inputs["state_gla"])[0]
    w_s = f(inputs["w_s"])[0]
    b_s = f(inputs["b_s"])[0]
    AH = c.AH
    wsT_p = np.ascontiguousarray(w_s.transpose(2, 0, 1).reshape(128, AH * 128))
    blk = np.zeros((AH, 128, 128), np.float32)
    blk[:, :64, :64] = w_s[:, :64, :64]
    blk[:, 64:, 64:] = w_s[:, :64, :64]
    wsT_s = np.ascontiguousarray(blk.transpose(2, 0, 1).reshape(128, AH * 128))
    bs_p = np.ascontiguousarray(b_s.reshape(1, AH * 128))
    bs_s = np.ascontiguousarray(np.concatenate([b_s[:, :64], b_s[:, :64]], axis=1).reshape(1, AH * 128))
    shared = dict(
        w_in=f(inputs["w_in"])[0], w_out=f(inputs["w_out"])[0], w_fg=f(inputs["w_ffn_gate"])[0], w_fu=f(inputs["w_ffn_up"])[0],
        w_fd=f(inputs["w_ffn_down"])[0],
        gmixc=np.ascontiguousarray(f(inputs["g_mix"])[0].reshape(c.KD, 128).T), gffnc=np.ascontiguousarray(f(inputs["g_ffn"])[0].reshape(c.KD, 128).T),
        gfin=f(inputs["g_final"]).reshape(1, c.D), ln_g=f(inputs["ln_g"]), ln_b=f(inputs["ln_b"]),
        wgu=np.ascontiguousarray(np.concatenate([f(inputs["w_gate_up"])[0], f(inputs["b_gate"])], axis=0)),
        gnorm=f(inputs["gla_norm_g"]), wsT_p=wsT_p, wsT_s=wsT_s, bs_p=bs_p, bs_s=bs_s, **host_consts(c))
    ntok = c.NPG * c.TG
    pre_tok = max(c.LBG, 1) * c.TG
    in_maps = []
    for core in range(c.NCORES):
        m = dict(shared)
        m["xp"] = np.ascontiguousarray(xp[core * ntok:(core + 1) * ntok])
        m["xs"] = np.ascontiguousarray(xs[core * c.NSEQ:(core + 1) * c.NSEQ].reshape(c.TG, c.D))
        m["st"] = np.ascontiguousarray(stt[core * c.NSEQ:(core + 1) * c.NSEQ])
        pre = np.zeros((pre_tok, c.D), np.float32)
        avail = min(core * ntok, c.LBG * c.TG)
        if avail > 0:
            pre[pre_tok - avail:] = xp[core * ntok - avail:core * ntok]
        m["xpre"] = pre
        in_maps.append(m)
    return in_maps


def run_cfg(cfg, inputs, key="main"):
    if key not in _NC_CACHE:
        _NC_CACHE[key] = MK(cfg).build()
    nc = _NC_CACHE[key]
    in_maps = make_in_maps(cfg, inputs)
    res = run_bass_kernel_spmd(nc, in_maps, core_ids=list(range(cfg.NCORES)))
    r = res.results
    c = cfg
    y_prompt = np.concatenate([r[i]["yp"] for i in range(c.NCORES)], axis=0)[None]
    y_sample = np.concatenate([r[i]["ys"].reshape(c.NSEQ, 64, c.D) for i in range(c.NCORES)], axis=0)
    new_gla_prompt = r[c.NCORES - 1]["sp_out"][None, None]
    new_gla_sample = np.concatenate([r[i]["ss_out"] for i in range(c.NCORES)], axis=0)[None]
    new_sgu = np.concatenate([r[i]["vs_out"].reshape(c.NSEQ, 64, c.AW) for i in range(c.NCORES)], axis=0)[None]
    return (y_prompt.astype(np.float32), y_sample.astype(np.float32), new_gla_prompt.astype(np.float32),
            new_gla_sample.astype(np.float32), new_sgu.astype(np.float32))


def kernel(**inputs):
    cfg = Cfg(D=4096, AH=16, BH=4, NPG=8, NSEQ=4, LBG=8, NCORES=8)
    return run_cfg(cfg, inputs)
```
